# Optimizing a Trainium2 kernel written in Bass

```python
import math
import jax
import jax.numpy as jnp
from jax import lax
import numpy as np

D_MODEL = 1024
BATCH = 8
SEQ = 8192
DEPTH = 1
DEC_BATCH = 8
DEC_SEQ = 32
PAST_LEN = 4096

CHUNK = 64
N_HEADS_A = 8
HEAD_DIM_A = 64
N_HEADS_IDX = 4
HEAD_DIM_IDX = 64
TOPK_MAX = 256
Q_BLOCK = CHUNK
NUM_BUCKETS = 32
MAX_DISTANCE = 1024
N_HEADS_R = 8
KEY_DIM_R = 64
VAL_DIM_R = 128
ROPE_BASE = 10000.0
D_FF = -(-8 * D_MODEL // (3 * 256)) * 256
ALPHA = (2.0 * DEPTH) ** 0.25
BETA = (8.0 * DEPTH) ** -0.25
LN_EPS = 1e-5
GN_EPS = 1e-6
W_A = N_HEADS_A * HEAD_DIM_A
W_IQ = N_HEADS_IDX * HEAD_DIM_IDX
W_RQK = N_HEADS_R * KEY_DIM_R
W_RV = N_HEADS_R * VAL_DIM_R
SPLIT_SIZES = (W_A, W_A, W_A, W_IQ, HEAD_DIM_IDX, N_HEADS_IDX, W_RQK, W_RQK, W_RV, W_RV, D_MODEL, D_MODEL)
D_IN = sum(SPLIT_SIZES)

kernel_name = 'dsa_retention_streaming_encoder'


def layer_norm(x, g, b, eps=LN_EPS):
    xf = x.astype(jnp.float32)
    mu = jnp.mean(xf, axis=-1, keepdims=True)
    var = jnp.mean(jnp.square(xf - mu), axis=-1, keepdims=True)
    y = (xf - mu) * lax.rsqrt(var + eps) * g.astype(jnp.float32) + b.astype(jnp.float32)
    return y.astype(x.dtype)


def project(x, w_in):
    h = jnp.einsum('btd,de->bte', x, w_in)
    offsets = tuple(int(v) for v in np.cumsum(SPLIT_SIZES)[:-1])
    return jnp.split(h, offsets, axis=-1)


def rotary(x, pos):
    half = x.shape[-1] // 2
    inv_freq = ROPE_BASE ** (-jnp.arange(half, dtype=jnp.float32) / half)
    ang = pos.astype(jnp.float32)[:, None] * inv_freq[None, :]
    cos = jnp.cos(ang)[None, :, None, :]
    sin = jnp.sin(ang)[None, :, None, :]
    xf = x.astype(jnp.float32)
    x1, x2 = xf[..., :half], xf[..., half:]
    return jnp.concatenate([x1 * cos - x2 * sin, x1 * sin + x2 * cos], axis=-1).astype(x.dtype)


def t5_bucket(rel):
    nb = NUM_BUCKETS // 2
    max_exact = nb // 2
    ret = jnp.where(rel > 0, nb, 0)
    n = jnp.abs(rel)
    nf = jnp.maximum(n, max_exact).astype(jnp.float32)
    large = max_exact + (jnp.log(nf / max_exact) / math.log(MAX_DISTANCE / max_exact) * (nb - max_exact)).astype(jnp.int32)
    large = jnp.minimum(large, nb - 1)
    return ret + jnp.where(n < max_exact, n, large)


def dsa_attend(q, qi, w_idx, qpos, k, v, k_idx, t5_bias, n_top):
    L = k.shape[1]
    kpos = jnp.arange(L, dtype=jnp.int32)
    rel = jax.nn.relu(jnp.einsum('bqhd,bsd->bqhs', qi, k_idx) * (HEAD_DIM_IDX ** -0.5))
    score = jnp.einsum('bqhs,bqh->bqs', rel, w_idx).astype(jnp.float32)
    admissible = (kpos[None, :] // CHUNK) <= (qpos[:, None] // CHUNK)
    score = jnp.where(admissible[None], score, -jnp.inf)
    top, idx = lax.top_k(score, n_top)
    valid = jnp.isfinite(top)
    gather = jax.vmap(lambda a, i: a[i])
    k_sel = gather(k, idx)
    v_sel = gather(v, idx)
    bias = t5_bias[t5_bucket(idx - qpos[None, :, None])]
    logits = jnp.einsum('bqhd,bqnhd->bqhn', q, k_sel).astype(jnp.float32) * (HEAD_DIM_A ** -0.5)
    logits = logits + jnp.moveaxis(bias, 3, 2).astype(jnp.float32)
    logits = jnp.where(valid[:, :, None, :], logits, -jnp.inf)
    p = jax.nn.softmax(logits, axis=-1).astype(v.dtype)
    return jnp.einsum('bqhn,bqnhd->bqhd', p, v_sel)


def retention_chunk(q, k, v, S, log_gamma):
    C = q.shape[2]
    n = jnp.arange(C, dtype=jnp.float32)
    diff = n[:, None] - n[None, :]
    lg = log_gamma[:, None, None]
    D = jnp.where(diff >= 0, jnp.exp(lg * jnp.maximum(diff, 0.0)), 0.0)
    qf, kf, vf = q.astype(jnp.float32), k.astype(jnp.float32), v.astype(jnp.float32)
    inner = jnp.einsum('bhnd,bhmd->bhnm', qf, kf) * D[None]
    q_dec = qf * jnp.exp(log_gamma[:, None] * (n + 1.0))[None, :, :, None]
    o = jnp.einsum('bhnm,bhme->bhne', inner, vf) + jnp.einsum('bhnd,bhde->bhne', q_dec, S)
    k_dec = kf * jnp.exp(log_gamma[:, None] * (C - 1.0 - n))[None, :, :, None]
    S_new = jnp.exp(log_gamma * C)[None, :, None, None] * S + jnp.einsum('bhmd,bhme->bhde', k_dec, vf)
    return o, S_new


def retention_output(o, gate, gn_g):
    B, H, T, dv = o.shape
    mu = jnp.mean(o, axis=-1, keepdims=True)
    var = jnp.mean(jnp.square(o - mu), axis=-1, keepdims=True)
    on = (o - mu) * lax.rsqrt(var + GN_EPS)
    on = on.transpose(0, 2, 1, 3).reshape(B, T, H * dv) * gn_g.astype(jnp.float32)
    return on.astype(gate.dtype) * jax.nn.silu(gate)


def split_inputs(x, pos, w_in, idx_g, idx_b):
    B, T, _ = x.shape
    qa, ka, va, qi, ki, wi, qr, kr, vr, gr, g_a, g_r = project(x, w_in)
    qa = qa.reshape(B, T, N_HEADS_A, HEAD_DIM_A)
    ka = ka.reshape(B, T, N_HEADS_A, HEAD_DIM_A)
    va = va.reshape(B, T, N_HEADS_A, HEAD_DIM_A)
    qi = qi.reshape(B, T, N_HEADS_IDX, HEAD_DIM_IDX)
    ki = layer_norm(ki, idx_g, idx_b)
    wi = wi * (N_HEADS_IDX ** -0.5)
    qr = rotary(qr.reshape(B, T, N_HEADS_R, KEY_DIM_R), pos).transpose(0, 2, 1, 3)
    kr = (rotary(kr.reshape(B, T, N_HEADS_R, KEY_DIM_R), pos) * (KEY_DIM_R ** -0.5)).transpose(0, 2, 1, 3)
    vr = vr.reshape(B, T, N_HEADS_R, VAL_DIM_R).transpose(0, 2, 1, 3)
    return qa, ka, va, qi, ki, wi, qr, kr, vr, gr, g_a, g_r


def finish_layer(x, a, o_ret, gr, g_a, g_r, ret_gn_g, w_pa, w_pr, w_o, ln1_g, ln1_b, w_gate, w_up, w_down, ln2_g, ln2_b):
    r = retention_output(o_ret, gr, ret_gn_g)
    merged = jax.nn.sigmoid(g_a) * (a @ w_pa) + jax.nn.sigmoid(g_r) * (r @ w_pr)
    x1 = layer_norm(ALPHA * x + merged @ w_o, ln1_g, ln1_b)
    h = jax.nn.silu(x1 @ w_gate) * (x1 @ w_up)
    return layer_norm(ALPHA * x1 + h @ w_down, ln2_g, ln2_b)


def prompt_sparse_attention(qa, ka, va, qi, ki, wi, t5_bias):
    B, T = qa.shape[:2]
    nb = T // Q_BLOCK
    n_top = min(TOPK_MAX, T // 4)

    def blockify(a):
        return jnp.moveaxis(a.reshape((B, nb, Q_BLOCK) + a.shape[2:]), 1, 0)

    qpos = jnp.arange(T, dtype=jnp.int32).reshape(nb, Q_BLOCK)

    def one_block(xs):
        q_b, qi_b, w_b, pos_b = xs
        return dsa_attend(q_b, qi_b, w_b, pos_b, ka, va, ki, t5_bias, n_top)

    out = lax.map(one_block, (blockify(qa), blockify(qi), blockify(wi), qpos))
    return jnp.moveaxis(out, 0, 1).reshape(B, T, W_A)


def prompt_retention(qr, kr, vr, log_gamma):
    B, H, T, _ = qr.shape
    nc = T // CHUNK

    def chunkify(a):
        return jnp.moveaxis(a.reshape(B, H, nc, CHUNK, a.shape[-1]), 2, 0)

    def step(S, xs):
        q_c, k_c, v_c = xs
        o_c, S_new = retention_chunk(q_c, k_c, v_c, S, log_gamma)
        return S_new, o_c

    S0 = jnp.zeros((B, H, KEY_DIM_R, VAL_DIM_R), jnp.float32)
    S_fin, o = lax.scan(step, S0, (chunkify(qr), chunkify(kr), chunkify(vr)))
    o = jnp.moveaxis(o, 0, 2).reshape(B, H, T, VAL_DIM_R)
    return o, S_fin


def setup_inputs(seed: int = 0) -> dict:
    key = jax.random.key(seed)
    ks = jax.random.split(key, 24)
    nrm = jax.random.normal
    f32 = jnp.float32
    return {
        'x_prompt': nrm(ks[0], (BATCH, SEQ, D_MODEL), f32),
        'x_sample': nrm(ks[1], (DEC_BATCH, DEC_SEQ, D_MODEL), f32),
        'cache_k': nrm(ks[2], (DEPTH, DEC_BATCH, PAST_LEN, N_HEADS_A, HEAD_DIM_A), f32),
        'cache_v': nrm(ks[3], (DEPTH, DEC_BATCH, PAST_LEN, N_HEADS_A, HEAD_DIM_A), f32),
        'cache_idx_k': nrm(ks[4], (DEPTH, DEC_BATCH, PAST_LEN, HEAD_DIM_IDX), f32),
        'state_ret': 0.5 * nrm(ks[5], (DEPTH, DEC_BATCH, N_HEADS_R, KEY_DIM_R, VAL_DIM_R), f32),
        'w_in': nrm(ks[6], (DEPTH, D_MODEL, D_IN), f32) * D_MODEL ** -0.5,
        'idx_k_norm_g': 1.0 + 0.02 * nrm(ks[7], (DEPTH, HEAD_DIM_IDX), f32),
        'idx_k_norm_b': 0.02 * nrm(ks[8], (DEPTH, HEAD_DIM_IDX), f32),
        't5_bias': 0.5 * nrm(ks[9], (NUM_BUCKETS, N_HEADS_A), f32),
        'ret_gn_g': 1.0 + 0.02 * nrm(ks[10], (DEPTH, W_RV), f32),
        'w_pa': nrm(ks[11], (DEPTH, W_A, D_MODEL), f32) * W_A ** -0.5,
        'w_pr': nrm(ks[12], (DEPTH, W_RV, D_MODEL), f32) * W_RV ** -0.5,
        'w_o': nrm(ks[13], (DEPTH, D_MODEL, D_MODEL), f32) * (D_MODEL ** -0.5 * BETA),
        'ln1_g': 1.0 + 0.02 * nrm(ks[14], (DEPTH, D_MODEL), f32),
        'ln1_b': 0.02 * nrm(ks[15], (DEPTH, D_MODEL), f32),
        'w_gate': nrm(ks[16], (DEPTH, D_MODEL, D_FF), f32) * D_MODEL ** -0.5,
        'w_up': nrm(ks[17], (DEPTH, D_MODEL, D_FF), f32) * D_MODEL ** -0.5,
        'w_down': nrm(ks[18], (DEPTH, D_FF, D_MODEL), f32) * (D_FF ** -0.5 * BETA),
        'ln2_g': 1.0 + 0.02 * nrm(ks[19], (DEPTH, D_MODEL), f32),
        'ln2_b': 0.02 * nrm(ks[20], (DEPTH, D_MODEL), f32),
    }


def reference(x_prompt, x_sample, cache_k, cache_v, cache_idx_k, state_ret, w_in, idx_k_norm_g, idx_k_norm_b,
              t5_bias, ret_gn_g, w_pa, w_pr, w_o, ln1_g, ln1_b, w_gate, w_up, w_down, ln2_g, ln2_b):
    log_gamma = jnp.log1p(-jnp.exp2(-5.0 - jnp.arange(N_HEADS_R, dtype=jnp.float32)))
    yp, ys = x_prompt, x_sample
    kp_l, vp_l, ikp_l, sp_l = [], [], [], []
    ks_l, vs_l, iks_l, ss_l = [], [], [], []
    for l in range(DEPTH):
        lw = (ret_gn_g[l], w_pa[l], w_pr[l], w_o[l], ln1_g[l], ln1_b[l], w_gate[l], w_up[l], w_down[l], ln2_g[l], ln2_b[l])

        T = yp.shape[1]
        pos_p = jnp.arange(T, dtype=jnp.int32)
        qa, ka, va, qi, ki, wi, qr, kr, vr, gr, g_a, g_r = split_inputs(yp, pos_p, w_in[l], idx_k_norm_g[l], idx_k_norm_b[l])
        a_p = prompt_sparse_attention(qa, ka, va, qi, ki, wi, t5_bias)
        o_p, s_p = prompt_retention(qr, kr, vr, log_gamma)
        yp_new = finish_layer(yp, a_p, o_p, gr, g_a, g_r, *lw)
        kp_l.append(ka)
        vp_l.append(va)
        ikp_l.append(ki)
        sp_l.append(s_p)

        past = cache_k.shape[2]
        Bs, Ts = ys.shape[:2]
        pos_s = past + jnp.arange(Ts, dtype=jnp.int32)
        qa_s, ka_s, va_s, qi_s, ki_s, wi_s, qr_s, kr_s, vr_s, gr_s, g_a_s, g_r_s = split_inputs(
            ys, pos_s, w_in[l], idx_k_norm_g[l], idx_k_norm_b[l])
        k_all = jnp.concatenate([cache_k[l], ka_s.astype(cache_k.dtype)], axis=1)
        v_all = jnp.concatenate([cache_v[l], va_s.astype(cache_v.dtype)], axis=1)
        ki_all = jnp.concatenate([cache_idx_k[l], ki_s.astype(cache_idx_k.dtype)], axis=1)
        L = past + Ts
        a_s = dsa_attend(qa_s, qi_s, wi_s, pos_s, k_all, v_all, ki_all, t5_bias, min(TOPK_MAX, L // 4)).reshape(Bs, Ts, W_A)
        o_s, s_s = retention_chunk(qr_s, kr_s, vr_s, state_ret[l].astype(jnp.float32), log_gamma)
        ys_new = finish_layer(ys, a_s, o_s, gr_s, g_a_s, g_r_s, *lw)
        ks_l.append(ka_s)
        vs_l.append(va_s)
        iks_l.append(ki_s)
        ss_l.append(s_s.astype(state_ret.dtype))

        yp, ys = yp_new, ys_new

    return (yp, ys,
            jnp.stack(kp_l), jnp.stack(vp_l), jnp.stack(ikp_l), jnp.stack(sp_l),
            jnp.stack(ks_l), jnp.stack(vs_l), jnp.stack(iks_l), jnp.stack(ss_l))
```

```python
import os
import numpy as np
from contextlib import ExitStack
SKIP = set(os.environ.get('KSKIP', '').split(','))
CUT = int(os.environ.get('KCUT', '99'))
import concourse.bass as bass
import concourse.mybir as mybir
from concourse.bass_utils import run_bass_kernel_spmd

F32 = mybir.dt.float32
BF16 = mybir.dt.bfloat16
ALU = mybir.AluOpType
AF = mybir.ActivationFunctionType
AX = mybir.AxisListType


class Buf:
    def __init__(self, t, name, dram=False):
        self.t = t
        self.name = name
        self.dram = dram
        self.w = {}
        self.r = {}
        self.si = None
        self.so = None
        self.ci = 0
        self.co = 0

    def __getitem__(self, k):
        return self.t[k]


class Sched:
    ENGS = ("pe", "act", "dve", "pool", "sp")
    EPOCH = 30000

    def __init__(self, nc, stack):
        self.nc = nc
        self.stack = stack
        self.sems = []
        self.ops = {e: [] for e in self.ENGS}
        self.cnt = {e: 0 for e in self.ENGS}
        self.own = {e: set() for e in self.ENGS}
        self.cur = {}
        self.seen = {e: {} for e in self.ENGS}
        self.latest = {}
        for _ in range(int(os.environ.get('KDUMMY', '0'))):
            self.new_sem('dummy')
        for e in self.ENGS:
            self._new_epoch(e)
        self.nops = 0

    def new_sem(self, name):
        h = self.stack.enter_context(self.nc.semaphore(f"{name}_{len(self.sems)}"))
        self.sems.append(h)
        return len(self.sems) - 1

    def _new_epoch(self, e):
        k = self.new_sem("e" + e)
        self.cur[e] = k
        self.own[e].add(k)
        self.cnt[e] = 0

    def _deps(self, reads, writes, skip=None):
        d = {}
        for b in reads:
            for k, v in b.w.items():
                if d.get(k, 0) < v:
                    d[k] = v
        for b in writes:
            if b.dram:
                continue
            for dic in (b.w, b.r):
                for k, v in dic.items():
                    if k == skip:
                        continue
                    if d.get(k, 0) < v:
                        d[k] = v
        return d

    def _waits(self, eng, d):
        seen = self.seen[eng]
        out = []
        for k, v in d.items():
            if eng == "pe" and k in self.own["pe"]:
                continue
            if seen.get(k, 0) >= v:
                continue
            seen[k] = v
            out.append((k, v))
        return out

    def op(self, eng, fn, reads=(), writes=()):
        d = self._deps(reads, writes)
        waits = self._waits(eng, d)
        if self.cnt[eng] >= self.EPOCH:
            self._new_epoch(eng)
        self.cnt[eng] += 1
        k, v = self.cur[eng], self.cnt[eng]
        self.latest[k] = v
        self.ops[eng].append((waits, fn, k, 1))
        self.nops += 1
        for b in reads:
            if not b.dram:
                b.r[k] = v
        for b in writes:
            if b.dram:
                b.w[k] = v
            else:
                b.w = {k: v}
                b.r = {}

    def dma(self, q, ob, oap, ib, iap, **kw):
        if not ob.dram:
            if ob.si is None or ob.ci >= 48000:
                ob.si, ob.ci = self.dma_sem(fresh=(q == "pool"))
            skip = ob.si
        else:
            if ib.so is None or ib.co >= 48000:
                ib.so, ib.co = self.dma_sem(fresh=(q == "pool"))
            skip = None
        d = self._deps([ib], [ob], skip=skip)
        waits = self._waits(q, d)
        if not ob.dram:
            ob.ci += 16
            k, v = ob.si, ob.ci
        else:
            ib.co += 16
            k, v = ib.so, ib.co
        self.latest[k] = v
        self.ops[q].append((waits, (lambda e: e.dma_start(out=oap, in_=iap, **kw)), k, 16))
        self.nops += 1
        if not ib.dram:
            ib.r[k] = v
        if ob.dram:
            ob.w[k] = v
        else:
            if ob.si in ob.w and len(ob.w) == 1:
                ob.w[k] = v
            else:
                ob.w = {k: v}
                ob.r = {}

    def dma_sem(self, fresh=False):
        fr = self.__dict__.setdefault("free_dma", [])
        while fr and not fresh:
            k = fr.pop()
            if self.latest.get(k, 0) < 40000:
                self.__dict__.setdefault("phase_dma", []).append(k)
                return k, self.latest.get(k, 0)
        k = self.new_sem("d")
        if not fresh:
            self.__dict__.setdefault("phase_dma", []).append(k)
        return k, 0

    def end_phase(self):
        self.__dict__.setdefault("free_dma", []).extend(self.__dict__.get("phase_dma", []))
        self.phase_dma = []

    def barrier(self):
        for e in self.ENGS:
            waits = self._waits(e, dict(self.latest))
            if waits:
                self.ops[e].append((waits, None, None, 0))

    def emit(self):
        nc = self.nc
        sems = self.sems
        engmap = {"pe": "tensor", "act": "scalar", "dve": "vector", "pool": "gpsimd", "sp": "sync"}
        with nc.Block() as block:
            for e in self.ENGS:
                lst = self.ops[e]

                def body(eng, lst=lst):
                    for waits, fn, k, inc in lst:
                        for wk, wv in waits:
                            eng.wait_ge(sems[wk], wv)
                        if fn is not None:
                            ins = fn(eng)
                            ins.then_inc(sems[k], inc)

                getattr(block, engmap[e])(body)
        self.ops = {e: [] for e in self.ENGS}


D = 1024
NH = 8
DH = 64
NHI = 4
DI = 64
NHR = 8
DKR = 64
DVR = 128
DFF = 2816
DIN = 6980
C_QA, C_KA, C_VA, C_QI, C_KI, C_WI, C_QR, C_KR, C_VR, C_GR, C_GA, C_GRR = (
    0, 512, 1024, 1536, 1792, 1856, 1860, 2372, 2884, 3908, 4932, 5956)
P1COLS = 3908
ALPHA = 2.0 ** 0.25
LN_EPS = 1e-5
GN_EPS = 1e-6
TOPK = 256
NEG = -30000.0


def dap(t, offset, dims):
    return bass.AP(tensor=t, offset=offset, ap=[list(d) for d in dims])


class Ctx:
    pass


def build_program(T, PAST, NS=32, phases=(1, 2, 3, 4), dbg=False):
    nc = bass.Bass("TRN2", target_bir_lowering=False)
    LS = PAST + NS
    NT = T // 128
    g = Ctx()
    g.nc = nc
    di = lambda n, s, dt=F32: nc.dram_tensor(n, s, dt, kind="ExternalInput")
    do = lambda n, s, dt=F32: nc.dram_tensor(n, s, dt, kind="ExternalOutput")
    dsx = lambda n, s, dt=BF16: nc.dram_tensor(n, s, dt, kind=("ExternalOutput" if dbg else "Internal"))
    I = {}
    for n, s in [("xp", [T, D]), ("xs", [NS, D]), ("ck", [PAST, 512]), ("cv", [PAST, 512]), ("cik", [PAST, 64]),
                 ("sr", [NHR * DKR, DVR]), ("w_in", [D, DIN]), ("idx_g", [1, 64]), ("idx_b", [1, 64]),
                 ("t5", [32, 8]), ("gn_g", [1, 1024]), ("w_pa", [512, D]), ("w_pr", [1024, D]), ("w_o", [D, D]),
                 ("ln1_g", [1, D]), ("ln1_b", [1, D]), ("w_gate", [D, DFF]), ("w_up", [D, DFF]), ("w_down", [DFF, D]),
                 ("ln2_g", [1, D]), ("ln2_b", [1, D]),
                 ("c_ident", [128, 128]), ("c_anti", [128, 128]), ("c_oh", [32, 896]), ("c_DT", [128, 8, 128]), ("c_Gq", [128, 4, 128]), ("c_Gk", [128, 2, 8]), ("c_gC", [128, 2, 8, 64]), ("c_cos", [T + NS, 32]), ("c_sin", [T + NS, 32])]:
        I[n] = Buf(di(n, s), n, True)
    O = {}
    for n, s in [("yp", [T, D]), ("ys", [NS, D]), ("kp", [T, 512]), ("vp", [T, 512]), ("ikp", [T, 64]),
                 ("stp", [NHR * DKR, DVR]), ("ks", [NS, 512]), ("vs", [NS, 512]), ("iks", [NS, 64]),
                 ("sts", [NHR * DKR, DVR])]:
        O[n] = Buf(do(n, s), n, True)
    X = {}
    TT = T + NS
    for n, s, dt in [("xT", [D, TT], BF16),
                     ("qaT", [128, 4, TT], BF16), ("qiT", [128, 2, TT], BF16), ("wi", [TT, 4], F32),
                     ("qrT", [128, 4, TT], BF16), ("krT", [128, 4, TT], BF16), ("kr", [TT, 512], BF16),
                     ("vr", [TT, 1024], BF16),
                     ("kaT_p", [128, 4, T], BF16), ("v_p", [T, 520], BF16), ("kiT_p", [64, T], BF16),
                     ("kaT_s", [128, 4, LS], BF16), ("v_s", [LS, 520], BF16), ("bvec", [8, 1024], BF16), ("kiT_s", [64, LS], BF16),
                     ("a", [TT, 512], F32), ("x1", [TT, D], F32), ("x1T", [D, TT], BF16)]:
        X[n] = Buf(dsx(n, s, dt), n, True)
    g.I, g.O, g.X = I, O, X
    g.T, g.PAST, g.NS, g.LS, g.NT, g.TT = T, PAST, NS, LS, NT, TT

    with ExitStack() as outer:
        S = Sched(nc, outer)
        g.S = S
        if 1 in phases:
            with ExitStack() as ph:
                phase1(g, ph)
                print('SBUF remaining after phase1:', nc.sbuf_bytes_remaining, flush=True)
                S.barrier()
                S.emit()
                S.end_phase()
        if 2 in phases:
            with ExitStack() as ph:
                phaseA(g, ph)
                print('SBUF remaining after phaseA:', nc.sbuf_bytes_remaining, flush=True)
                S.barrier()
                S.emit()
                S.end_phase()
        if 3 in phases:
            with ExitStack() as ph:
                phaseF1(g, ph)
                print('SBUF remaining after phaseF1:', nc.sbuf_bytes_remaining, flush=True)
                S.barrier()
                S.emit()
                S.end_phase()
        if 4 in phases:
            with ExitStack() as ph:
                phaseF2(g, ph)
                print('SBUF remaining after phaseF2:', nc.sbuf_bytes_remaining, flush=True)
                S.barrier()
                S.emit()
                S.end_phase()
        S.barrier()
        S.emit()
        global LAST_NOPS
        LAST_NOPS = (S.nops, len(S.sems), {e: len(S.own[e]) for e in S.ENGS})
    return nc


def mk_alloc(g, ph):
    nc = g.nc
    if not hasattr(g, 'alloc_cnt'):
        g.alloc_cnt = [0]
    cnt = g.alloc_cnt

    def sb(shape, dt, name):
        cnt[0] += 1
        return Buf(ph.enter_context(nc.sbuf_tensor(f"{name}_{cnt[0]}", shape, dt)), name)

    def ps(shape, dt, name):
        cnt[0] += 1
        return Buf(ph.enter_context(nc.psum_tensor(f"{name}_{cnt[0]}", shape, dt)), name)

    return sb, ps


def phase1(g, ph):
    S, I, O, X = g.S, g.I, g.O, g.X
    T, PAST, NS, NT = g.T, g.PAST, g.NS, g.NT
    sb, ps = mk_alloc(g, ph)
    w16 = sb([128, 8, P1COLS], BF16, "w16")
    for kc in range(8):
        S.dma("pool", w16, w16[:, kc, :], I["w_in"], I["w_in"].t.ap()[kc * 128:(kc + 1) * 128, 0:P1COLS])
    id16 = sb([128, 128], BF16, "id16")
    S.dma("pool", id16, id16[:, :], I["c_ident"], I["c_ident"].t.ap()[:, :])
    gb = sb([128, 2, 64], F32, "gb")
    S.dma("sp", gb, gb[:, 0, :], I["idx_g"], dap(I["idx_g"].t, 0, [[0, 128], [1, 64]]))
    S.dma("sp", gb, gb[:, 1, :], I["idx_b"], dap(I["idx_b"].t, 0, [[0, 128], [1, 64]]))
    eps = sb([128, 1], F32, "eps")
    S.op("dve", lambda e: e.memset(eps[:, :], LN_EPS), writes=[eps])
    NB = 3
    x16 = [sb([128, D], BF16, "x16") for _ in range(NB)]
    xT16 = [sb([128, 8, 128], BF16, "xT16") for _ in range(NB)]
    h32 = [sb([128, P1COLS], F32, "h32") for _ in range(NB)]
    q16 = [sb([128, 2368], BF16, "q16") for _ in range(NB)]
    fT16 = [sb([128, 19, 128], BF16, "fT16") for _ in range(NB)]
    v16 = [sb([128, 520], BF16, "v16") for _ in range(NB)]
    for vb in v16:
        S.op("pool", lambda e, vb=vb: e.memset(vb[:, :], 1.0), writes=[vb])
    vr16 = [sb([128, 1024], BF16, "vr16") for _ in range(NB)]
    cs = [sb([128, 2, 32], F32, "cs") for _ in range(NB)]
    rt = [sb([128, 4, 16, 32], F32, "rt") for _ in range(NB)]
    st6 = [sb([128, 6], F32, "st6") for _ in range(NB)]
    mv = [sb([128, 4], F32, "mv") for _ in range(NB)]
    kin = [sb([128, 64], F32, "kin") for _ in range(NB)]
    wi = [sb([128, 4], F32, "wi") for _ in range(NB)]
    pxT = [ps([128, 1024], BF16, "pxT") for _ in range(1)]
    pmm = [ps([128, 512], F32, "pmm") for _ in range(4)]
    pfT = [ps([128, 1024], BF16, "pfT") for _ in range(3)]
    mmi = 0

    def p1_loads(t2):
        smp2 = t2 == NT
        n2 = NS if smp2 else 128
        b2 = t2 % NB
        xsrc2 = I["xs"] if smp2 else I["xp"]
        xrow2 = 0 if smp2 else t2 * 128
        S.dma("pool", x16[b2], x16[b2][0:n2, :], xsrc2, xsrc2.t.ap()[xrow2:xrow2 + n2, :])
        S.dma("sp", cs[b2], cs[b2][0:n2, 0, :], I["c_cos"], I["c_cos"].t.ap()[t2 * 128:t2 * 128 + n2, :])
        S.dma("sp", cs[b2], cs[b2][0:n2, 1, :], I["c_sin"], I["c_sin"].t.ap()[t2 * 128:t2 * 128 + n2, :])

    for tt in range(NT + 1):
        smp = tt == NT
        n = NS if smp else 128
        b = tt % NB
        tok0 = tt * 128
        xsrc = I["xs"] if smp else I["xp"]
        xrow = 0 if smp else tok0
        for t2 in ([0, 1, 2] if tt == 0 else [tt + 2]):
            if t2 <= NT:
                p1_loads(t2)
        px = pxT[0]

        def f_xt(e, b=b, n=n, px=px):
            ins = None
            for kc in range(8):
                ins = e.transpose(px[:, kc * 128:kc * 128 + n], x16[b][0:n, kc * 128:(kc + 1) * 128], id16[0:n, 0:n])
            return ins
        S.op("pe", f_xt, reads=[x16[b], id16], writes=[px])
        S.op("dve", lambda e, b=b, n=n, px=px: e.tensor_copy(
            out=xT16[b][:, :, 0:n], in_=px[:, :].rearrange("p (k t) -> p k t", k=8)[:, :, 0:n]),
            reads=[px], writes=[xT16[b]])
        S.dma("sp", X["xT"], X["xT"].t.ap().rearrange("(k p) t -> p k t", p=128)[:, :, tok0:tok0 + n],
              xT16[b], xT16[b][:, :, 0:n])
        nblk = (P1COLS + 511) // 512
        for cb in range(nblk):
            c0 = cb * 512
            cw = min(512, P1COLS - c0)
            pm = pmm[mmi % 4]
            mmi += 1

            def f_mm(e, b=b, n=n, pm=pm, c0=c0, cw=cw):
                ins = None
                for kc in range(8):
                    ins = e.matmul(pm[0:n, 0:cw], lhsT=xT16[b][:, kc, 0:n], rhs=w16[:, kc, c0:c0 + cw],
                                   start=(kc == 0), stop=(kc == 7))
                return ins
            S.op("pe", f_mm, reads=[xT16[b], w16], writes=[pm])
            if cb % 2 == 0:
                S.op("act", lambda e, b=b, n=n, pm=pm, c0=c0, cw=cw: e.activation(
                    out=h32[b][0:n, c0:c0 + cw], in_=pm[0:n, 0:cw], func=AF.Copy), reads=[pm], writes=[h32[b]])
            else:
                S.op("dve", lambda e, b=b, n=n, pm=pm, c0=c0, cw=cw: e.tensor_copy(
                    out=h32[b][0:n, c0:c0 + cw], in_=pm[0:n, 0:cw]), reads=[pm], writes=[h32[b]])
        H = h32[b]
        ko, vo, iko = (O["ks"], O["vs"], O["iks"]) if smp else (O["kp"], O["vp"], O["ikp"])
        orow = 0 if smp else tok0
        S.dma("sp", ko, ko.t.ap()[orow:orow + n, :], H, H[0:n, C_KA:C_KA + 512])
        S.dma("sp", vo, vo.t.ap()[orow:orow + n, :], H, H[0:n, C_VA:C_VA + 512])
        S.op("dve", lambda e, b=b, n=n, H=H: e.bn_stats(out=st6[b][0:n, :], in_=H[0:n, C_KI:C_KI + 64]),
             reads=[H], writes=[st6[b]])
        S.op("dve", lambda e, b=b, n=n: e.bn_aggr(out=mv[b][0:n, 0:2], in_=st6[b][0:n, :]),
             reads=[st6[b]], writes=[mv[b]])
        S.op("act", lambda e, b=b, n=n: e.activation(out=mv[b][0:n, 2:3], in_=mv[b][0:n, 1:2], func=AF.Sqrt,
                                                    bias=eps[0:n, 0:1], scale=1.0), reads=[mv[b], eps], writes=[mv[b]])
        S.op("dve", lambda e, b=b, n=n: e.reciprocal(out=mv[b][0:n, 3:4], in_=mv[b][0:n, 2:3]),
             reads=[mv[b]], writes=[mv[b]])
        S.op("dve", lambda e, b=b, n=n, H=H: e.tensor_scalar(
            out=kin[b][0:n, :], in0=H[0:n, C_KI:C_KI + 64], scalar1=mv[b][0:n, 0:1], scalar2=mv[b][0:n, 3:4],
            op0=ALU.subtract, op1=ALU.mult), reads=[H, mv[b]], writes=[kin[b]])
        S.op("dve", lambda e, b=b, n=n: e.tensor_tensor(out=kin[b][0:n, :], in0=kin[b][0:n, :], in1=gb[0:n, 0, :],
                                                       op=ALU.mult), reads=[kin[b], gb], writes=[kin[b]])
        S.op("dve", lambda e, b=b, n=n: e.tensor_tensor(out=kin[b][0:n, :], in0=kin[b][0:n, :], in1=gb[0:n, 1, :],
                                                       op=ALU.add), reads=[kin[b], gb], writes=[kin[b]])
        S.dma("sp", iko, iko.t.ap()[orow:orow + n, :], kin[b], kin[b][0:n, :])
        S.op("pool", lambda e, b=b, n=n, H=H: e.tensor_scalar(out=wi[b][0:n, :], in0=H[0:n, C_WI:C_WI + 4],
                                                             scalar1=0.0625, scalar2=None, op0=ALU.mult),
             reads=[H], writes=[wi[b]])
        S.dma("sp", X["wi"], X["wi"].t.ap()[tok0:tok0 + n, :], wi[b], wi[b][0:n, :])
        Q = q16[b]
        S.op("pool", lambda e, n=n, H=H, Q=Q: e.tensor_copy(out=Q[0:n, 0:1024], in_=H[0:n, C_QA:C_QA + 1024]),
             reads=[H], writes=[Q])
        S.op("pool", lambda e, n=n, H=H, Q=Q: e.tensor_copy(out=Q[0:n, 1024:1280], in_=H[0:n, C_QI:C_QI + 256]),
             reads=[H], writes=[Q])
        S.op("pool", lambda e, b=b, n=n, Q=Q: e.tensor_copy(out=Q[0:n, 1280:1344], in_=kin[b][0:n, :]),
             reads=[kin[b]], writes=[Q])
        S.op("act", lambda e, b=b, n=n, H=H: e.activation(out=v16[b][0:n, :].rearrange("p (h d) -> p h d", d=65)[:, :, 0:64], in_=H[0:n, C_VA:C_VA + 512].rearrange("p (h d) -> p h d", d=64), func=AF.Copy),
             reads=[H], writes=[v16[b]])
        S.op("act", lambda e, b=b, n=n, H=H: e.activation(out=vr16[b][0:n, :], in_=H[0:n, C_VR:C_VR + 1024], func=AF.Copy),
             reads=[H], writes=[vr16[b]])
        R = rt[b]
        xv = H[0:n, C_QR:C_QR + 1024].rearrange("p (h d) -> p h d", h=16)
        x1, x2 = xv[:, :, 0:32], xv[:, :, 32:64]
        cb_ = cs[b][0:n, 0:1, :].to_broadcast([n, 16, 32])
        sb_ = cs[b][0:n, 1:2, :].to_broadcast([n, 16, 32])
        ov = Q[0:n, 1344:2368].rearrange("p (h d) -> p h d", h=16)
        for j, (xa, tb) in enumerate([(x1, cb_), (x2, sb_), (x1, sb_), (x2, cb_)]):
            eng = "dve" if j % 2 == 0 else "pool"
            S.op(eng, lambda e, j=j, xa=xa, tb=tb, R=R, n=n: e.tensor_tensor(out=R[0:n, j, :, :], in0=xa, in1=tb, op=ALU.mult),
                 reads=[H, cs[b]], writes=[R])
        S.op("dve", lambda e, R=R, n=n, ov=ov: e.tensor_tensor(out=ov[:, :, 0:32], in0=R[0:n, 0, :, :], in1=R[0:n, 1, :, :],
                                                                op=ALU.subtract), reads=[R], writes=[Q])
        S.op("pool", lambda e, R=R, n=n, ov=ov: e.tensor_tensor(out=ov[:, :, 32:64], in0=R[0:n, 2, :, :], in1=R[0:n, 3, :, :],
                                                                 op=ALU.add), reads=[R], writes=[Q])
        vdst = X["v_s"] if smp else X["v_p"]
        vrow = PAST if smp else tok0
        S.dma("sp", vdst, vdst.t.ap()[vrow:vrow + n, :], v16[b], v16[b][0:n, :])
        S.dma("sp", X["vr"], X["vr"].t.ap()[tok0:tok0 + n, :], vr16[b], vr16[b][0:n, :])
        S.dma("sp", X["kr"], X["kr"].t.ap()[tok0:tok0 + n, :], Q, Q[0:n, 1856:2368])
        srcs = [(i * 128, 128) for i in range(10)] + [(1344 + i * 128, 128) for i in range(8)] + [(1280, 64)]
        F = fT16[b]
        for gi in range(3):
            pf = pfT[gi]
            sl = list(range(gi * 8, min(19, gi * 8 + 8)))

            def f_ft(e, sl=sl, pf=pf, n=n, Q=Q):
                ins = None
                for j, si in enumerate(sl):
                    c0, cw = srcs[si]
                    ins = e.transpose(pf[0:cw, j * 128:j * 128 + n], Q[0:n, c0:c0 + cw], id16[0:n, 0:n])
                return ins
            S.op("pe", f_ft, reads=[Q, id16], writes=[pf])
            nfull = len(sl) if gi < 2 else 2
            pv = pf[:, 0:nfull * 128].rearrange("p (k t) -> p k t", t=128)[:, :, 0:n]
            if gi == 1:
                S.op("act", lambda e, sl=sl, pv=pv, n=n, F=F, nfull=nfull: e.activation(
                    out=F[:, sl[0]:sl[0] + nfull, 0:n], in_=pv, func=AF.Copy), reads=[pf], writes=[F])
            else:
                S.op("dve", lambda e, sl=sl, pv=pv, n=n, F=F, nfull=nfull: e.tensor_copy(
                    out=F[:, sl[0]:sl[0] + nfull, 0:n], in_=pv), reads=[pf], writes=[F])
            if gi == 2:
                S.op("dve", lambda e, pf=pf, n=n, F=F: e.tensor_copy(out=F[0:64, 18, 0:n], in_=pf[0:64, 256:256 + n]),
                     reads=[pf], writes=[F])
        kdst = X["kaT_s"] if smp else X["kaT_p"]
        kidst = X["kiT_s"] if smp else X["kiT_p"]
        kcol = PAST if smp else tok0
        S.dma("sp", X["qaT"], X["qaT"].t.ap()[:, :, tok0:tok0 + n], F, F[:, 0:4, 0:n])
        S.dma("sp", kdst, kdst.t.ap()[:, :, kcol:kcol + n], F, F[:, 4:8, 0:n])
        S.dma("sp", X["qiT"], X["qiT"].t.ap()[:, :, tok0:tok0 + n], F, F[:, 8:10, 0:n])
        S.dma("sp", kidst, kidst.t.ap()[:, kcol:kcol + n], F, F[0:64, 18, 0:n])
        S.dma("sp", X["qrT"], X["qrT"].t.ap()[:, :, tok0:tok0 + n], F, F[:, 10:14, 0:n])
        S.dma("sp", X["krT"], X["krT"].t.ap()[:, :, tok0:tok0 + n], F, F[:, 14:18, 0:n])

    ck16 = [sb([128, 576], BF16, "ck16") for _ in range(2)]
    cv16 = [sb([128, 520], BF16, "cv16") for _ in range(2)]
    cF = [sb([128, 5, 128], BF16, "cF") for _ in range(2)]
    for vb in cv16:
        S.op("pool", lambda e, vb=vb: e.memset(vb[:, :], 1.0), writes=[vb])
    for j in range(PAST // 128):
        b = j % 2
        r0 = j * 128
        S.dma("pool", ck16[b], ck16[b][:, 0:512], I["ck"], I["ck"].t.ap()[r0:r0 + 128, :])
        S.dma("pool", ck16[b], ck16[b][:, 512:576], I["cik"], I["cik"].t.ap()[r0:r0 + 128, :])
        S.dma("pool", cv16[b], cv16[b][:, :].rearrange("p (h d) -> p h d", d=65)[:, :, 0:64],
              I["cv"], I["cv"].t.ap()[r0:r0 + 128, :].rearrange("p (h d) -> p h d", d=64))
        S.dma("sp", X["v_s"], X["v_s"].t.ap()[r0:r0 + 128, :], cv16[b], cv16[b][:, :])
        pf = pfT[j % 3]

        def f_ct(e, b=b, pf=pf):
            ins = None
            for i in range(4):
                ins = e.transpose(pf[:, i * 128:(i + 1) * 128], ck16[b][:, i * 128:(i + 1) * 128], id16[:, :])
            ins = e.transpose(pf[0:64, 512:640], ck16[b][:, 512:576], id16[:, :])
            return ins
        S.op("pe", f_ct, reads=[ck16[b], id16], writes=[pf])
        S.op("dve", lambda e, b=b, pf=pf: e.tensor_copy(out=cF[b][:, 0:4, :], in_=pf[:, 0:512].rearrange("p (k t) -> p k t", t=128)),
             reads=[pf], writes=[cF[b]])
        S.op("act", lambda e, b=b, pf=pf: e.activation(out=cF[b][0:64, 4, :], in_=pf[0:64, 512:640], func=AF.Copy),
             reads=[pf], writes=[cF[b]])
        S.dma("sp", X["kaT_s"], X["kaT_s"].t.ap()[:, :, r0:r0 + 128], cF[b], cF[b][:, 0:4, :])
        S.dma("sp", X["kiT_s"], X["kiT_s"].t.ap()[:, r0:r0 + 128], cF[b], cF[b][0:64, 4, :])


def t5_bucket_np(rel):
    nb = 16
    max_exact = 8
    ret = np.where(rel > 0, nb, 0)
    n = np.abs(rel)
    nf = np.maximum(n, max_exact).astype(np.float32)
    large = max_exact + (np.log(nf / np.float32(max_exact)) / np.float32(np.log(1024 / max_exact)) * np.float32(nb - max_exact)).astype(np.int32)
    large = np.minimum(large, nb - 1)
    return ret + np.where(n < max_exact, n, large)


RLEN = 1024
LAST_NOPS = None


def phaseA(g, ph):
    S, I, O, X = g.S, g.I, g.O, g.X
    T, PAST, NS, NT, LS = g.T, g.PAST, g.NS, g.NT, g.LS
    sb, ps = mk_alloc(g, ph)
    NIT = int(os.environ.get("KNIT", "20"))
    LMAX = max(T, LS)
    NKT_MAX = (LMAX + 127) // 128
    QBMAX = 512
    id16 = sb([128, 128], BF16, "id16")
    anti16 = sb([128, 128], BF16, "anti16")
    S.dma("pool", id16, id16[:, :], I["c_ident"], I["c_ident"].t.ap()[:, :])
    S.dma("pool", anti16, anti16[:, :], I["c_anti"], I["c_anti"].t.ap()[:, :])
    pS = [ps([128, 512], F32, "pS") for _ in range(3)]
    pT = ps([128, 1024], BF16, "pT")
    pacc = [ps([128, 512], F32, "pacc") for _ in range(4)]
    t5sb = sb([32, 8], F32, "t5sb")
    oh = sb([32, 896], F32, "oh")
    S.dma("sp", t5sb, t5sb[:, :], I["t5"], I["t5"].t.ap()[:, :])
    S.dma("sp", oh, oh[:, :], I["c_oh"], I["c_oh"].t.ap()[:, :])
    bv16 = sb([8, 896], BF16, "bv16")
    for hf in range(2):
        pb = pS[hf]
        S.op("pe", lambda e, pb=pb, hf=hf: e.matmul(pb[0:8, 0:448], lhsT=t5sb[:, :], rhs=oh[:, hf * 448:(hf + 1) * 448],
                                                    start=True, stop=True), reads=[t5sb, oh], writes=[pb])
        S.op("act", lambda e, pb=pb, hf=hf: e.activation(out=bv16[:, hf * 448:(hf + 1) * 448], in_=pb[0:8, 0:448],
                                                        func=AF.Copy, scale=8.0), reads=[pb], writes=[bv16])
    S.dma("sp", X["bvec"], X["bvec"].t.ap()[:, 0:896], bv16, bv16[:, :])
    Mt = sb([128, 6, 8, 128], BF16, "Mt")
    for dl in range(6):
        S.dma("sp", Mt, Mt[:, dl, :, :], X["bvec"], dap(X["bvec"].t, 128 * dl, [[1, 128], [RLEN, 8], [1, 128]]))
    cfar = sb([128, 8], F32, "cfar")
    S.dma("sp", cfar, cfar[:, :], I["t5"], dap(I["t5"].t, 15 * 8, [[0, 128], [1, 8]]))
    cfar8 = sb([128, 8], F32, "cfar8")
    S.op("dve", lambda e: e.tensor_scalar(out=cfar8[:, :], in0=cfar[:, :], scalar1=8.0, scalar2=None, op0=ALU.mult),
         reads=[cfar], writes=[cfar8])
    Mfar = sb([128, 8, 128], BF16, "Mfar")
    S.op("dve", lambda e: e.tensor_copy(out=Mfar[:, :, :], in_=cfar8[:, :].unsqueeze(2).to_broadcast([128, 8, 128])),
         reads=[cfar8], writes=[Mfar])
    zero1 = sb([128, 1], F32, "zero1")
    S.op("dve", lambda e: e.memset(zero1[:, :], 0.0), writes=[zero1])
    pw2 = sb([128, NIT], F32, "pw2")
    for it in range(NIT):
        S.op("dve", lambda e, it=it: e.memset(pw2[:, it:it + 1], 0.5 ** it), writes=[pw2])
    acc = sb([128, LMAX], F32, "acc")
    nm16 = sb([128, LMAX], BF16, "nm16")
    negT2 = [sb([128, NKT_MAX, 256], BF16, "negT") for _ in range(2)]
    rh = [sb([128, 512], F32, "rh") for _ in range(4)]
    ki2 = [sb([128, 512], BF16, "ki2") for _ in range(2)]
    qi_t = [sb([128, 2, 128], BF16, "qi_t") for _ in range(2)]
    wi_t = [sb([128, 4], F32, "wi_t") for _ in range(2)]
    sc = [sb([128, 8], F32, "sc") for _ in range(2)]
    wv = [sb([128, NIT], F32, "wv") for _ in range(2)]
    qa_b = [sb([128, 4, QBMAX], BF16, "qa_b") for _ in range(2)]
    Kc = [sb([128, 2, 512], BF16, "Kc") for _ in range(3)]
    Vc = [sb([128, 4, 4, 65], BF16, "Vc") for _ in range(3)]
    PT = [sb([128, QBMAX], BF16, "PT") for _ in range(3)]
    a32 = [sb([128, 4, 512], F32, "a32") for _ in range(2)]
    rs = [sb([128, 4, 8], F32, "rs") for _ in range(2)]
    cnts = {"mm": 0, "rh": 0, "ki": 0, "kc": 0, "pt": 0, "qt": 0}

    def group(smp):
        if smp:
            QB, npq, nblk, tokbase = NS, NS, 1, T
            kaT, vS, kiT, Ltot = X["kaT_s"], X["v_s"], X["kiT_s"], LS
        else:
            QB, npq, nblk, tokbase = 256, 128, T // 256, 0
            kaT, vS, kiT, Ltot = X["kaT_p"], X["v_p"], X["kiT_p"], T
        nsub = QB // npq
        topk = min(TOPK, Ltot // 4)
        def genA(bi):
            q0 = tokbase + bi * QB
            nkeys = Ltot if smp else (bi + 1) * QB
            ktiles = [(k, min(128, nkeys - k)) for k in range(0, nkeys, 128)]
            nkt = len(ktiles)
            NT_ = negT2[bi % 2]
            for ms in range(nsub):
                qt = cnts["qt"] % 2
                cnts["qt"] += 1
                tq0 = q0 + ms * npq
                Lm = Ltot if smp else 128 * (nsub * bi + ms + 1)
                S.dma("sp", qi_t[qt], qi_t[qt][:, :, 0:npq], X["qiT"], X["qiT"].t.ap()[:, :, tq0:tq0 + npq])
                S.dma("sp", wi_t[qt], wi_t[qt][0:npq, :], X["wi"], X["wi"].t.ap()[tq0:tq0 + npq, :])
                for k0 in range(0, Lm, 512):
                    kn = min(512, Lm - k0)
                    kb = ki2[cnts["ki"] % 2]
                    cnts["ki"] += 1
                    S.dma("sp", kb, kb[0:64, 0:kn], kiT, kiT.t.ap()[:, k0:k0 + kn])
                    S.dma("sp", kb, kb[64:128, 0:kn], kiT, kiT.t.ap()[:, k0:k0 + kn])
                    for h in range(4):
                        pm = pS[cnts["mm"] % 3]
                        cnts["mm"] += 1
                        r = rh[cnts["rh"] % 4]
                        cnts["rh"] += 1
                        pb0 = (h % 2) * 64
                        S.op("pe", lambda e, pm=pm, qt=qt, h=h, pb0=pb0, kb=kb, kn=kn: e.matmul(
                            pm[0:npq, 0:kn], lhsT=qi_t[qt][pb0:pb0 + 64, h // 2, 0:npq], rhs=kb[pb0:pb0 + 64, 0:kn],
                            start=True, stop=True), reads=[qi_t[qt], kb], writes=[pm])
                        S.op("act", lambda e, pm=pm, r=r, kn=kn: e.activation(out=r[0:npq, 0:kn], in_=pm[0:npq, 0:kn], func=AF.Relu),
                             reads=[pm], writes=[r])
                        eng = "dve"
                        if h == 0:
                            S.op(eng, lambda e, r=r, qt=qt, k0=k0, kn=kn: e.tensor_scalar(
                                out=acc[0:npq, k0:k0 + kn], in0=r[0:npq, 0:kn], scalar1=wi_t[qt][0:npq, 0:1], scalar2=None,
                                op0=ALU.mult), reads=[r, wi_t[qt]], writes=[acc])
                        else:
                            S.op(eng, lambda e, r=r, qt=qt, k0=k0, kn=kn, h=h: e.scalar_tensor_tensor(
                                out=acc[0:npq, k0:k0 + kn], in0=r[0:npq, 0:kn], scalar=wi_t[qt][0:npq, h:h + 1],
                                in1=acc[0:npq, k0:k0 + kn], op0=ALU.mult, op1=ALU.add), reads=[r, wi_t[qt], acc], writes=[acc])
                s_ = sc[qt]
                w_ = wv[qt]
                S.op("dve", lambda e, s_=s_, Lm=Lm: e.tensor_reduce(out=s_[0:npq, 0:1], in_=acc[0:npq, 0:Lm], axis=AX.X, op=ALU.max,
                                                                    apply_absolute_value=True), reads=[acc], writes=[s_])
                if not smp:
                    S.op("dve", lambda e, Lm=Lm: e.memset(acc[0:64, Lm - 64:Lm], -1.0e30), reads=[s_], writes=[acc])
                S.op("dve", lambda e, s_=s_: e.tensor_scalar(out=s_[0:npq, 0:1], in0=s_[0:npq, 0:1], scalar1=1.0, scalar2=None,
                                                             op0=ALU.add), reads=[s_], writes=[s_])
                S.op("dve", lambda e, s_=s_: e.tensor_scalar(out=s_[0:npq, 1:2], in0=s_[0:npq, 0:1], scalar1=-1.0, scalar2=None,
                                                             op0=ALU.mult), reads=[s_], writes=[s_])
                S.op("dve", lambda e, s_=s_, w_=w_: e.tensor_scalar(out=w_[0:npq, :], in0=pw2[0:npq, :], scalar1=s_[0:npq, 0:1],
                                                                    scalar2=None, op0=ALU.mult), reads=[s_, pw2], writes=[w_])
                S.op("dve", lambda e, s_=s_: e.memset(s_[0:npq, 2:3], 0.0), writes=[s_])
                for it in range(NIT):
                    S.op("dve", lambda e, s_=s_, Lm=Lm: e.tensor_scalar(
                        out=nm16[0:npq, 0:Lm], in0=acc[0:npq, 0:Lm], scalar1=s_[0:npq, 2:3], scalar2=0.0, op0=ALU.is_ge,
                        op1=ALU.add, accum_out=s_[0:npq, 3:4]), reads=[acc, s_], writes=[nm16, s_])
                    S.op("dve", lambda e, s_=s_, w_=w_, it=it: e.tensor_scalar(
                        out=s_[0:npq, 4:5], in0=s_[0:npq, 3:4], scalar1=topk - 0.5, scalar2=w_[0:npq, it:it + 1],
                        op0=ALU.is_ge, op1=ALU.mult), reads=[s_, w_], writes=[s_])
                    wc = it + 1 if it < NIT - 1 else it
                    oc = 2 if it < NIT - 1 else 1
                    S.op("dve", lambda e, s_=s_, w_=w_, wc=wc, oc=oc: e.scalar_tensor_tensor(
                        out=s_[0:npq, oc:oc + 1], in0=s_[0:npq, 4:5], scalar=w_[0:npq, wc:wc + 1], in1=s_[0:npq, 2:3],
                        op0=ALU.subtract, op1=ALU.add), reads=[s_, w_], writes=[s_])
                S.op("dve", lambda e, s_=s_, Lm=Lm: e.tensor_scalar(
                    out=nm16[0:npq, 0:Lm], in0=acc[0:npq, 0:Lm], scalar1=s_[0:npq, 1:2], scalar2=NEG, op0=ALU.is_lt,
                    op1=ALU.mult), reads=[acc, s_], writes=[nm16])
                yield
                mt = [(k, w) for (k, w) in ktiles if k < Lm]
                for g0 in range(0, len(mt), 8):
                    grp = mt[g0:g0 + 8]

                    def f_tr(e, grp=grp):
                        ins = None
                        for j, (k, w) in enumerate(grp):
                            ins = e.transpose(pT[0:w, j * 128:j * 128 + npq], nm16[0:npq, k:k + w], id16[0:npq, 0:npq])
                        return ins
                    S.op("pe", f_tr, reads=[nm16, id16], writes=[pT])
                    full = [x for x in grp if x[1] == 128]
                    if full:
                        nf = len(full)
                        S.op("act", lambda e, g0=g0, nf=nf, ms=ms: e.activation(
                            out=NT_[:, g0:g0 + nf, ms * npq:(ms + 1) * npq],
                            in_=pT[:, 0:nf * 128].rearrange("p (k t) -> p k t", t=128)[:, :, 0:npq], func=AF.Copy),
                            reads=[pT], writes=[NT_])
                    if len(full) < len(grp):
                        j = len(grp) - 1
                        k, w = grp[j]
                        S.op("act", lambda e, g0=g0, j=j, w=w, ms=ms: e.activation(
                            out=NT_[0:w, g0 + j, ms * npq:(ms + 1) * npq], in_=pT[0:w, j * 128:j * 128 + npq], func=AF.Copy),
                            reads=[pT], writes=[NT_])
                if len(mt) < nkt:
                    S.op("pool", lambda e, a=len(mt), ms=ms, nkt=nkt: e.memset(NT_[:, a:nkt, ms * npq:(ms + 1) * npq], NEG), writes=[NT_])

        def genB(bi):
            q0 = tokbase + bi * QB
            nkeys = Ltot if smp else (bi + 1) * QB
            ktiles = [(k, min(128, nkeys - k)) for k in range(0, nkeys, 128)]
            nkt = len(ktiles)
            NT_ = negT2[bi % 2]
            pa = pacc[(bi % 2) * 2:(bi % 2) * 2 + 2]
            qb = qa_b[bi % 2]
            S.dma("sp", qb, qb[:, :, 0:QB], X["qaT"], X["qaT"].t.ap()[:, :, q0:q0 + QB])
            A32 = a32[bi % 2]
            for hg in range(2):
                pend = []
                for c0 in range(0, nkt, 4):
                    ch = ktiles[c0:c0 + 4]
                    kc = Kc[cnts["kc"] % 3]
                    vc = Vc[cnts["kc"] % 3]
                    cnts["kc"] += 1
                    kk0 = ch[0][0]
                    kkn = sum(w for _, w in ch)
                    S.dma("sp", kc, kc[:, :, 0:kkn], kaT, kaT.t.ap()[:, 2 * hg:2 * hg + 2, kk0:kk0 + kkn])
                    for j, (k, w) in enumerate(ch):
                        S.dma("sp", vc, vc[0:w, j, :, :], vS,
                              vS.t.ap()[k:k + w, hg * 260:(hg + 1) * 260].rearrange("p (h d) -> p h d", d=65))
                    for hl in range(4):
                        h = hg * 4 + hl
                        pb0 = (h % 2) * 64
                        for j, (k, w) in enumerate(ch):
                            jt = c0 + j
                            pm = pS[cnts["mm"] % 3]
                            cnts["mm"] += 1
                            if smp:
                                deltas = [jt - PAST // 128]
                            else:
                                deltas = [jt - (nsub * bi + ms) for ms in range(nsub)]
                            allfar = all(dl <= -6 for dl in deltas)

                            def f_s(e, pm=pm, kc=kc, qb=qb, pb0=pb0, hl=hl, j=j, w=w, jt=jt, h=h, deltas=deltas, allfar=allfar):
                                e.matmul(pm[0:w, 0:QB], lhsT=kc[pb0:pb0 + 64, hl // 2, j * 128:j * 128 + w],
                                         rhs=qb[pb0:pb0 + 64, h // 2, 0:QB], start=True, stop=False)
                                ins = None
                                if not allfar:
                                    for ms, dl in enumerate(deltas):
                                        if dl > 0:
                                            continue
                                        rhs = Mfar[:, h, 0:npq] if dl <= -6 else Mt[:, -dl, h, 0:npq]
                                        e.matmul(pm[0:w, ms * npq:(ms + 1) * npq], lhsT=anti16[:, 0:w], rhs=rhs, start=False, stop=False)
                                ins = e.matmul(pm[0:w, 0:QB], lhsT=id16[0:w, 0:w], rhs=NT_[0:w, jt, 0:QB], start=False, stop=True)
                                return ins
                            S.op("pe", f_s, reads=[kc, qb, NT_, Mt, Mfar, id16, anti16], writes=[pm])
                            pt = PT[cnts["pt"] % 3]
                            cnts["pt"] += 1
                            bias_ap = cfar[0:w, h:h + 1] if allfar else zero1[0:w, 0:1]
                            S.op("act", lambda e, pm=pm, pt=pt, w=w, bias_ap=bias_ap: e.activation(
                                out=pt[0:w, 0:QB], in_=pm[0:w, 0:QB], func=AF.Exp, bias=bias_ap, scale=0.125),
                                reads=[pm, cfar, zero1], writes=[pt])

                            def f_pv(e, pt=pt, vc=vc, hl=hl, j=j, w=w, jt=jt, nkt=nkt):
                                ins = None
                                for ms in range(nsub):
                                    ins = e.matmul(pa[ms][0:npq, hl * 65:(hl + 1) * 65], lhsT=pt[0:w, ms * npq:(ms + 1) * npq],
                                                   rhs=vc[0:w, j, hl, :], start=(jt == 0 and hl == 0), stop=(jt == nkt - 1 and hl == 3))
                                return ins
                            pend.append((f_pv, [pt, vc]))
                            if len(pend) > 2:
                                fq, rq = pend.pop(0)
                                S.op("pe", fq, reads=rq, writes=pa[0:nsub])
                while pend:
                    fq, rq = pend.pop(0)
                    S.op("pe", fq, reads=rq, writes=pa[0:nsub])
                R = rs[bi % 2]
                for ms in range(nsub):
                    pv = pa[ms][0:npq, 0:260].rearrange("p (h d) -> p h d", d=65)
                    S.op("dve", lambda e, pv=pv, R=R, ms=ms, hg=hg: e.reciprocal(out=R[0:npq, ms, hg * 4:hg * 4 + 4], in_=pv[:, :, 64]),
                         reads=[pa[ms]], writes=[R])
                    S.op("dve", lambda e, pv=pv, R=R, ms=ms, hg=hg, A32=A32: e.tensor_tensor(
                        out=A32[0:npq, ms, hg * 256:(hg + 1) * 256].rearrange("p (h d) -> p h d", d=64), in0=pv[:, :, 0:64],
                        in1=R[0:npq, ms, hg * 4:hg * 4 + 4].unsqueeze(2).to_broadcast([npq, 4, 64]), op=ALU.mult),
                        reads=[pa[ms], R], writes=[A32])
                if hg == 0:
                    yield
            for ms in range(nsub):
                r0 = q0 + ms * npq
                S.dma("pool", X["a"], X["a"].t.ap()[r0:r0 + npq, :], A32, A32[0:npq, ms, :])

        prevB = None
        for bi in range(nblk):
            ga = genA(bi)
            for _st in range(nsub):
                next(ga)
                if prevB is not None:
                    next(prevB, None)
            for _ in ga:
                pass
            if prevB is not None:
                for _ in prevB:
                    pass
            prevB = genB(bi)
        for _ in prevB:
            pass

    group(False)
    group(True)


def phaseF1(g, ph):
    S, I, O, X = g.S, g.I, g.O, g.X
    T, PAST, NS, NT = g.T, g.PAST, g.NS, g.NT
    sb, ps = mk_alloc(g, ph)
    GC = 3072
    wg16 = sb([128, 8, GC], BF16, "wg16")
    wpa16 = sb([128, 4, D], BF16, "wpa16")
    wpr16 = sb([128, 8, D], BF16, "wpr16")
    wo16 = sb([128, 8, D], BF16, "wo16")
    for kc in range(8):
        S.dma("pool", wg16, wg16[:, kc, :], I["w_in"], I["w_in"].t.ap()[kc * 128:(kc + 1) * 128, C_GR:C_GR + GC])
        S.dma("pool", wpr16, wpr16[:, kc, :], I["w_pr"], I["w_pr"].t.ap()[kc * 128:(kc + 1) * 128, :])
        S.dma("pool", wo16, wo16[:, kc, :], I["w_o"], I["w_o"].t.ap()[kc * 128:(kc + 1) * 128, :])
    for kc in range(4):
        S.dma("pool", wpa16, wpa16[:, kc, :], I["w_pa"], I["w_pa"].t.ap()[kc * 128:(kc + 1) * 128, :])
    id16 = sb([128, 128], BF16, "id16")
    id32 = sb([128, 128], F32, "id32")
    S.dma("pool", id16, id16[:, :], I["c_ident"], I["c_ident"].t.ap()[:, :])
    S.dma("sp", id32, id32[:, :], I["c_ident"], I["c_ident"].t.ap()[:, :])
    DT = sb([128, 8, 128], F32, "DT")
    S.dma("sp", DT, DT[:, :, :], I["c_DT"], I["c_DT"].t.ap()[:, :, :])
    Gq = sb([128, 4, 128], F32, "Gq")
    S.dma("sp", Gq, Gq[:, :, :], I["c_Gq"], I["c_Gq"].t.ap()[:, :, :])
    Gk = sb([128, 2, 8], F32, "Gk")
    S.dma("sp", Gk, Gk[:, :, :], I["c_Gk"], I["c_Gk"].t.ap()[:, :, :])
    gC = sb([128, 2, 8, 64], F32, "gC")
    S.dma("sp", gC, gC[:, :, :, :], I["c_gC"], I["c_gC"].t.ap()[:, :, :, :])
    bc = sb([128, 3, D], F32, "bc")
    for i, nme in enumerate(["gn_g", "ln1_g", "ln1_b"]):
        S.dma("sp", bc, bc[:, i, :], I[nme], dap(I[nme].t, 0, [[0, 128], [1, D]]))
    eps = sb([128, 2], F32, "eps")
    S.op("dve", lambda e: e.memset(eps[:, 0:1], LN_EPS), writes=[eps])
    S.op("dve", lambda e: e.memset(eps[:, 1:2], GN_EPS), writes=[eps])
    ST = sb([128, 8, 64], F32, "ST")
    ST16 = sb([128, 8, 64], BF16, "ST16")
    S16 = sb([128, 4, 128], BF16, "S16")
    S32 = sb([128, 4, 128], F32, "S32")
    xT16_s = [sb([128, 8, 128], BF16, "xT16") for _ in range(2)]
    x32_s = [sb([128, D], F32, "x32") for _ in range(2)]
    a32_s = [sb([128, 512], F32, "a32") for _ in range(2)]
    a16_s = [sb([128, 512], BF16, "a16") for _ in range(2)]
    aT16_s = [sb([128, 4, 128], BF16, "aT16") for _ in range(2)]
    qrT_s = [sb([128, 4, 128], BF16, "qrT") for _ in range(2)]
    krT_s = [sb([128, 4, 128], BF16, "krT") for _ in range(2)]
    kr16_s = [sb([128, 512], BF16, "kr16") for _ in range(2)]
    vr16_s = [sb([128, 1024], BF16, "vr16") for _ in range(2)]
    qdec_s = [sb([128, 4, 128], BF16, "qdec") for _ in range(2)]
    kdec_s = [sb([128, 512], BF16, "kdec") for _ in range(2)]
    inD_s = [sb([128, 8, 128], BF16, "inD") for _ in range(2)]
    osq_s = [sb([128, D], F32, "osq") for _ in range(2)]
    on_s = [sb([128, D], F32, "on") for _ in range(2)]
    sl = sb([128, D], F32, "sl")
    st_s = [sb([128, 8, 8], F32, "st") for _ in range(2)]
    r16 = sb([128, D], BF16, "r16")
    rT16 = sb([128, 8, 128], BF16, "rT16")
    sgm = sb([128, D], F32, "sgm")
    mAm = sb([128, D], F32, "mAm")
    m16 = sb([128, D], BF16, "m16")
    mT16 = sb([128, 8, 128], BF16, "mT16")
    u = sb([128, D], F32, "u")
    u16 = sb([128, D], BF16, "u16")
    uT16 = sb([128, 8, 128], BF16, "uT16")
    st6 = sb([128, 2, 6], F32, "st6")
    mv = sb([128, 4], F32, "mv")
    pg = [ps([128, 512], F32, "pg") for _ in range(6)]
    pb = [ps([128, 1024], BF16, "pb") for _ in range(2)]
    pc = [0]

    def bank():
        pc[0] += 1
        return pg[pc[0] % 6]

    def transposes(src, ncols, dst, n, eng="dve"):
        nk = ncols // 128
        for g0 in range(0, nk, 8):
            pt = pb[(pc[0]) % 2]
            pc[0] += 1
            cnt = min(8, nk - g0)

            def f(e, pt=pt, g0=g0, cnt=cnt):
                ins = None
                for j in range(cnt):
                    ins = e.transpose(pt[:, j * 128:j * 128 + n], src[0:n, (g0 + j) * 128:(g0 + j + 1) * 128], id16[0:n, 0:n])
                return ins
            S.op("pe", f, reads=[src, id16], writes=[pt])
            pv = pt[:, 0:cnt * 128].rearrange("p (k t) -> p k t", t=128)[:, :, 0:n]
            if eng == "act":
                S.op("act", lambda e, pv=pv, g0=g0, cnt=cnt: e.activation(out=dst[:, g0:g0 + cnt, 0:n], in_=pv, func=AF.Copy),
                     reads=[pt], writes=[dst])
            else:
                S.op("dve", lambda e, pv=pv, g0=g0, cnt=cnt: e.tensor_copy(out=dst[:, g0:g0 + cnt, 0:n], in_=pv),
                     reads=[pt], writes=[dst])

    def ln_rows(src, n, gi, bi_, dst, epscol):
        for hf in range(2):
            S.op("dve", lambda e, hf=hf: e.bn_stats(out=st6[0:n, hf, :], in_=src[0:n, hf * 512:(hf + 1) * 512]),
                 reads=[src], writes=[st6])
        S.op("dve", lambda e: e.bn_aggr(out=mv[0:n, 0:2], in_=st6[0:n, :, :].rearrange("p a b -> p (a b)")), reads=[st6], writes=[mv])
        S.op("act", lambda e: e.activation(out=mv[0:n, 2:3], in_=mv[0:n, 1:2], func=AF.Sqrt, bias=eps[0:n, epscol:epscol + 1], scale=1.0),
             reads=[mv, eps], writes=[mv])
        S.op("dve", lambda e: e.reciprocal(out=mv[0:n, 3:4], in_=mv[0:n, 2:3]), reads=[mv], writes=[mv])
        S.op("dve", lambda e: e.tensor_scalar(out=dst[0:n, :], in0=src[0:n, :], scalar1=mv[0:n, 0:1], scalar2=mv[0:n, 3:4],
                                              op0=ALU.subtract, op1=ALU.mult), reads=[src, mv], writes=[dst])
        S.op("pool", lambda e: e.tensor_tensor(out=dst[0:n, :], in0=dst[0:n, :], in1=bc[0:n, gi, :], op=ALU.mult), reads=[dst, bc], writes=[dst])
        S.op("pool", lambda e: e.tensor_tensor(out=dst[0:n, :], in0=dst[0:n, :], in1=bc[0:n, bi_, :], op=ALU.add), reads=[dst, bc], writes=[dst])

    def write_state(dst):
        if 'ws' in SKIP:
            return
        for pr in range(4):
            pt = bank()
            S.op("pe", lambda e, pt=pt, pr=pr: e.transpose(pt[:, 0:128], ST[:, 2 * pr:2 * pr + 2, :].rearrange("p a b -> p (a b)"), id32[:, :]),
                 reads=[ST, id32], writes=[pt])
            S.op("dve", lambda e, pt=pt, pr=pr: e.tensor_copy(out=S32[:, pr, :], in_=pt[:, 0:128]), reads=[pt], writes=[S32])
        S.dma("sp", dst, dst.t.ap().rearrange("(pr p) e -> p pr e", p=128), S32, S32[:, :, :])

    S.op("dve", lambda e: e.memset(ST[:, :, :], 0.0), writes=[ST])
    S.op("dve", lambda e: e.memset(S16[:, :, :], 0.0), writes=[S16])
    pend_st = []

    def tile(tt):
        xT16 = xT16_s[tt % 2]
        osq = osq_s[tt % 2]
        on = on_s[tt % 2]
        st = st_s[tt % 2]
        x32 = x32_s[tt % 2]
        a32 = a32_s[tt % 2]
        a16 = a16_s[tt % 2]
        aT16 = aT16_s[tt % 2]
        qrT = qrT_s[tt % 2]
        krT = krT_s[tt % 2]
        kr16 = kr16_s[tt % 2]
        vr16 = vr16_s[tt % 2]
        qdec = qdec_s[tt % 2]
        kdec = kdec_s[tt % 2]
        inD = inD_s[tt % 2]
        smp = tt == NT
        n = NS if smp else 128
        tok0 = tt * 128
        ci = 1 if smp else 0
        if smp and 'ss' not in SKIP:
            write_state(O["stp"])
            S.dma("sp", S32, S32[:, :, :], I["sr"], I["sr"].t.ap().rearrange("(pr p) e -> p pr e", p=128))
            S.op("dve", lambda e: e.tensor_copy(out=S16[:, :, :], in_=S32[:, :, :]), reads=[S32], writes=[S16])
            for pr in range(4):
                pt = bank()
                S.op("pe", lambda e, pt=pt, pr=pr: e.transpose(pt[:, 0:128], S32[:, pr, :], id32[:, :]), reads=[S32, id32], writes=[pt])
                S.op("dve", lambda e, pt=pt, pr=pr: e.tensor_copy(out=ST[:, 2 * pr:2 * pr + 2, :].rearrange("p a b -> p (a b)"), in_=pt[:, 0:128]),
                     reads=[pt], writes=[ST])
        xsrc = I["xs"] if smp else I["xp"]
        xrow = 0 if smp else tok0
        S.dma("sp", x32, x32[0:n, :], xsrc, xsrc.t.ap()[xrow:xrow + n, :])
        S.dma("sp", xT16, xT16[:, :, 0:n], X["xT"], X["xT"].t.ap().rearrange("(k p) t -> p k t", p=128)[:, :, tok0:tok0 + n])
        S.dma("sp", a32, a32[0:n, :], X["a"], X["a"].t.ap()[tok0:tok0 + n, :])
        S.dma("sp", qrT, qrT[:, :, 0:n], X["qrT"], X["qrT"].t.ap()[:, :, tok0:tok0 + n])
        S.dma("sp", krT, krT[:, :, 0:n], X["krT"], X["krT"].t.ap()[:, :, tok0:tok0 + n])
        S.dma("sp", kr16, kr16[0:n, :], X["kr"], X["kr"].t.ap()[tok0:tok0 + n, :])
        S.dma("sp", vr16, vr16[0:n, :], X["vr"], X["vr"].t.ap()[tok0:tok0 + n, :])
        if CUT == 1:
            return
        S.op("pool", lambda e: e.tensor_tensor(out=qdec[:, :, 0:n], in0=qrT[:, :, 0:n], in1=Gq[:, :, 0:n], op=ALU.mult),
             reads=[qrT, Gq], writes=[qdec])
        while pend_st:
            oap_, iap_ = pend_st.pop(0)
            S.dma("pool", X["x1T"], oap_, uT16, iap_)
        if CUT == 11:
            return
        S.op("dve", lambda e, ci=ci: e.tensor_tensor(
            out=kdec[0:n, :].rearrange("p (h d) -> p h d", d=64), in0=kr16[0:n, :].rearrange("p (h d) -> p h d", d=64),
            in1=Gk[0:n, ci, :].unsqueeze(2).to_broadcast([n, 8, 64]), op=ALU.mult), reads=[kr16, Gk], writes=[kdec])
        if CUT == 12:
            return
        for hg in range(2):
            pis = [bank(), bank()]

            def f_in(e, pis=pis, hg=hg):
                ins = None
                for par in range(2):
                    for i2 in range(2):
                        h = hg * 4 + i2 * 2 + par
                        b0 = par * 64
                        ins = e.matmul(pis[par][0:n, i2 * 128:i2 * 128 + n], lhsT=krT[b0:b0 + 64, h // 2, 0:n],
                                       rhs=qrT[b0:b0 + 64, h // 2, 0:n], start=True, stop=True)
                return ins
            S.op("pe", f_in, reads=[krT, qrT], writes=pis)
            for par in range(2):
                h0 = hg * 4 + par
                S.op("dve", lambda e, pis=pis, par=par, h0=h0: e.tensor_tensor(
                    out=inD[0:n, h0:h0 + 3:2, 0:n], in0=pis[par][0:n, 0:256].rearrange("p (h t) -> p h t", t=128)[:, :, 0:n],
                    in1=DT[0:n, h0:h0 + 3:2, 0:n], op=ALU.mult), reads=[pis[par], DT], writes=[inD])
        if CUT in (13, 131):
            return
        po = [bank(), bank()]
        for hg in range(2):
            def f_o(e, hg=hg):
                ins = None
                for hl in range(4):
                    h = hg * 4 + hl
                    b0 = (h % 2) * 64
                    e.matmul(po[hg][0:n, hl * 128:(hl + 1) * 128], lhsT=inD[0:n, h, 0:n], rhs=vr16[0:n, h * 128:(h + 1) * 128],
                             start=True, stop=False)
                    ins = e.matmul(po[hg][0:n, hl * 128:(hl + 1) * 128], lhsT=qdec[b0:b0 + 64, h // 2, 0:n],
                                   rhs=S16[b0:b0 + 64, h // 2, :], start=False, stop=True)
                return ins
            S.op("pe", f_o, reads=[inD, vr16, qdec, S16], writes=[po[hg]])
        if CUT == 2:
            return
        pu = bank()

        def f_u(e):
            ins = None
            for h in range(8):
                ins = e.matmul(pu[:, h * 64:(h + 1) * 64], lhsT=vr16[0:n, h * 128:(h + 1) * 128], rhs=kdec[0:n, h * 64:(h + 1) * 64],
                               start=True, stop=True)
            return ins
        S.op("pe", f_u, reads=[vr16, kdec], writes=[pu])
        S.op("pool", lambda e, ci=ci: e.tensor_tensor(out=ST[:, :, :], in0=ST[:, :, :], in1=gC[:, ci, :, :], op=ALU.mult),
             reads=[ST, gC], writes=[ST])
        S.op("dve", lambda e: e.tensor_tensor(out=ST[:, :, :].rearrange("p a b -> p (a b)"), in0=ST[:, :, :].rearrange("p a b -> p (a b)"),
                                              in1=pu[:, :], op=ALU.add), reads=[ST, pu], writes=[ST])
        S.op("pool", lambda e: e.tensor_copy(out=ST16[:, :, :], in_=ST[:, :, :]), reads=[ST], writes=[ST16])
        if CUT == 3:
            return
        if not smp:
            for hp in range(1):
                pt = pb[pc[0] % 2]
                pc[0] += 1

                def f_st(e, pt=pt):
                    ins = None
                    for pr in range(4):
                        ins = e.transpose(pt[:, pr * 128:(pr + 1) * 128], ST16[:, 2 * pr:2 * pr + 2, :].rearrange("p a b -> p (a b)"), id16[:, :])
                    return ins
                S.op("pe", f_st, reads=[ST16, id16], writes=[pt])
                S.op("act", lambda e, pt=pt: e.activation(out=S16[:, :, :], in_=pt[:, 0:512].rearrange("p (k t) -> p k t", t=128), func=AF.Copy),
                     reads=[pt], writes=[S16])
        if CUT == 5:
            return
        for hg in range(2):
            ov = po[hg][0:n, :].rearrange("p (h e) -> p h e", e=128)
            S.op("act", lambda e, hg=hg: e.activation(out=on[0:n, hg * 512:(hg + 1) * 512], in_=po[hg][0:n, :], func=AF.Copy),
                 reads=[po[hg]], writes=[on])
            S.op("dve", lambda e, hg=hg: e.tensor_reduce(out=st[0:n, 0, hg * 4:hg * 4 + 4],
                                                         in_=on[0:n, hg * 512:(hg + 1) * 512].rearrange("p (h e) -> p h e", e=128),
                                                         axis=AX.X, op=ALU.add), reads=[on], writes=[st])
            S.op("act", lambda e, hg=hg: e.activation(out=osq[0:n, hg * 512:(hg + 1) * 512], in_=po[hg][0:n, :], func=AF.Square),
                 reads=[po[hg]], writes=[osq])
            S.op("dve", lambda e, hg=hg: e.tensor_reduce(out=st[0:n, 1, hg * 4:hg * 4 + 4],
                                                         in_=osq[0:n, hg * 512:(hg + 1) * 512].rearrange("p (h e) -> p h e", e=128),
                                                         axis=AX.X, op=ALU.add), reads=[osq], writes=[st])
        yield
        while pend_st:
            oap_, iap_ = pend_st.pop(0)
            S.dma("pool", X["x1T"], oap_, uT16, iap_)
        S.op("dve", lambda e: e.tensor_scalar(out=st[0:n, 2, :], in0=st[0:n, 0, :], scalar1=1.0 / 128, scalar2=None, op0=ALU.mult),
             reads=[st], writes=[st])
        S.op("dve", lambda e: e.tensor_tensor(out=st[0:n, 3, :], in0=st[0:n, 2, :], in1=st[0:n, 2, :], op=ALU.mult), reads=[st], writes=[st])
        S.op("dve", lambda e: e.scalar_tensor_tensor(out=st[0:n, 4, :], in0=st[0:n, 1, :], scalar=1.0 / 128, in1=st[0:n, 3, :],
                                                     op0=ALU.mult, op1=ALU.subtract), reads=[st], writes=[st])
        S.op("act", lambda e: e.activation(out=st[0:n, 5, :], in_=st[0:n, 4, :], func=AF.Sqrt, bias=eps[0:n, 1:2], scale=1.0),
             reads=[st, eps], writes=[st])
        S.op("dve", lambda e: e.reciprocal(out=st[0:n, 6, :], in_=st[0:n, 5, :]), reads=[st], writes=[st])
        for hg in range(2):
            ov = po[hg][0:n, :].rearrange("p (h e) -> p h e", e=128)
            onv = on[0:n, hg * 512:(hg + 1) * 512].rearrange("p (h e) -> p h e", e=128)
            S.op("dve", lambda e, onv=onv, hg=hg: e.tensor_tensor(
                out=onv, in0=onv, in1=st[0:n, 2, hg * 4:hg * 4 + 4].unsqueeze(2).to_broadcast([n, 4, 128]), op=ALU.subtract),
                reads=[on, st], writes=[on])
            S.op("dve", lambda e, onv=onv, hg=hg: e.tensor_tensor(
                out=onv, in0=onv, in1=st[0:n, 6, hg * 4:hg * 4 + 4].unsqueeze(2).to_broadcast([n, 4, 128]), op=ALU.mult),
                reads=[on, st], writes=[on])
        S.op("pool", lambda e: e.tensor_tensor(out=on[0:n, :], in0=on[0:n, :], in1=bc[0:n, 0, :], op=ALU.mult), reads=[on, bc], writes=[on])
        if CUT == 4:
            return
        for hf in range(2):
            pq = bank()

            def f_gr(e, pq=pq, hf=hf):
                ins = None
                for kc in range(8):
                    ins = e.matmul(pq[0:n, :], lhsT=xT16[:, kc, 0:n], rhs=wg16[:, kc, hf * 512:(hf + 1) * 512], start=(kc == 0), stop=(kc == 7))
                return ins
            S.op("pe", f_gr, reads=[xT16, wg16], writes=[pq])
            S.op("act", lambda e, pq=pq, hf=hf: e.activation(out=sl[0:n, hf * 512:(hf + 1) * 512], in_=pq[0:n, :], func=AF.Silu),
                 reads=[pq], writes=[sl])
        S.op("dve", lambda e: e.tensor_tensor(out=r16[0:n, :], in0=on[0:n, :], in1=sl[0:n, :], op=ALU.mult), reads=[on, sl], writes=[r16])
        transposes(r16, 1024, rT16, n, "act")
        S.op("pool", lambda e: e.tensor_copy(out=a16[0:n, :], in_=a32[0:n, :]), reads=[a32], writes=[a16])
        transposes(a16, 512, aT16, n, "dve")
        for bi_, (gcol, wmat, nkc, actT) in enumerate([(1024, wpa16, 4, aT16), (2048, wpr16, 8, rT16)]):
            for hf in range(2):
                pgt = bank()
                pp = bank()

                def f_g(e, pgt=pgt, hf=hf, gcol=gcol):
                    ins = None
                    c0 = gcol + hf * 512
                    for kc in range(8):
                        ins = e.matmul(pgt[0:n, :], lhsT=xT16[:, kc, 0:n], rhs=wg16[:, kc, c0:c0 + 512], start=(kc == 0), stop=(kc == 7))
                    return ins
                S.op("pe", f_g, reads=[wg16, xT16], writes=[pgt])

                def f_p(e, pp=pp, hf=hf, wmat=wmat, nkc=nkc, actT=actT):
                    ins = None
                    for kc in range(nkc):
                        ins = e.matmul(pp[0:n, :], lhsT=actT[:, kc, 0:n], rhs=wmat[:, kc, hf * 512:(hf + 1) * 512],
                                       start=(kc == 0), stop=(kc == nkc - 1))
                    return ins
                S.op("pe", f_p, reads=[wmat, actT], writes=[pp])
                cs_ = slice(hf * 512, (hf + 1) * 512)
                S.op("act", lambda e, pgt=pgt, cs_=cs_: e.activation(out=sgm[0:n, cs_], in_=pgt[0:n, :], func=AF.Sigmoid),
                     reads=[pgt], writes=[sgm])
                if bi_ == 0:
                    S.op("dve", lambda e, pp=pp, cs_=cs_: e.tensor_tensor(out=mAm[0:n, cs_], in0=sgm[0:n, cs_], in1=pp[0:n, :], op=ALU.mult),
                         reads=[sgm, pp], writes=[mAm])
                else:
                    S.op("dve", lambda e, pp=pp, cs_=cs_: e.tensor_tensor(out=sgm[0:n, cs_], in0=sgm[0:n, cs_], in1=pp[0:n, :], op=ALU.mult),
                         reads=[sgm, pp], writes=[sgm])
                    S.op("pool", lambda e, cs_=cs_: e.tensor_tensor(out=m16[0:n, cs_], in0=sgm[0:n, cs_], in1=mAm[0:n, cs_], op=ALU.add),
                         reads=[sgm, mAm], writes=[m16])
        transposes(m16, 1024, mT16, n, "act")
        for hf in range(2):
            pw = bank()

            def f_w(e, pw=pw, hf=hf):
                ins = None
                for kc in range(8):
                    ins = e.matmul(pw[0:n, :], lhsT=mT16[:, kc, 0:n], rhs=wo16[:, kc, hf * 512:(hf + 1) * 512], start=(kc == 0), stop=(kc == 7))
                return ins
            S.op("pe", f_w, reads=[mT16, wo16], writes=[pw])
            S.op("dve", lambda e, pw=pw, hf=hf: e.scalar_tensor_tensor(
                out=u[0:n, hf * 512:(hf + 1) * 512], in0=x32[0:n, hf * 512:(hf + 1) * 512], scalar=ALPHA, in1=pw[0:n, :],
                op0=ALU.mult, op1=ALU.add), reads=[x32, pw], writes=[u])
        ln_rows(u, n, 1, 2, u, 0)
        S.dma("pool", X["x1"], X["x1"].t.ap()[tok0:tok0 + n, :], u, u[0:n, :])
        S.op("act", lambda e: e.activation(out=u16[0:n, :], in_=u[0:n, :], func=AF.Copy), reads=[u], writes=[u16])
        transposes(u16, 1024, uT16, n, "dve")
        pend_st.append((X["x1T"].t.ap().rearrange("(k p) t -> p k t", p=128)[:, :, tok0:tok0 + n], uT16[:, :, 0:n]))
    g_prev = None
    for tt in range(NT + 1):
        g_cur = tile(tt)
        next(g_cur, None)
        if g_prev is not None:
            for _ in g_prev:
                pass
        g_prev = g_cur
    for _ in g_prev:
        pass
    while pend_st:
        oap_, iap_ = pend_st.pop(0)
        S.dma("pool", X["x1T"], oap_, uT16, iap_)
    write_state(O["sts"])


def phaseF2(g, ph):
    S, I, O, X = g.S, g.I, g.O, g.X
    T, PAST, NS, NT = g.T, g.PAST, g.NS, g.NT
    sb, ps = mk_alloc(g, ph)
    NFC = DFF // 128
    wg = sb([128, 8, DFF], BF16, "wg")
    wu = sb([128, 8, DFF], BF16, "wu")
    wd = sb([128, NFC, D], BF16, "wd")
    for kc in range(8):
        S.dma("pool", wg, wg[:, kc, :], I["w_gate"], I["w_gate"].t.ap()[kc * 128:(kc + 1) * 128, :])
        S.dma("pool", wu, wu[:, kc, :], I["w_up"], I["w_up"].t.ap()[kc * 128:(kc + 1) * 128, :])
    for fc in range(NFC):
        S.dma("pool", wd, wd[:, fc, :], I["w_down"], I["w_down"].t.ap()[fc * 128:(fc + 1) * 128, :])
    bc = sb([128, 2, D], F32, "bc2")
    for i, nme in enumerate(["ln2_g", "ln2_b"]):
        S.dma("sp", bc, bc[:, i, :], I[nme], dap(I[nme].t, 0, [[0, 128], [1, D]]))
    eps = sb([128, 1], F32, "eps2")
    S.op("dve", lambda e: e.memset(eps[:, :], LN_EPS), writes=[eps])
    BLK = 512
    x1T = [sb([128, 8, BLK], BF16, "x1T") for _ in range(2)]
    hT = [sb([128, NFC, BLK], BF16, "hT") for _ in range(1)]
    sgt = [sb([128, BLK], F32, "sgt") for _ in range(2)]
    x1 = [sb([128, D], F32, "x1") for _ in range(2)]
    st6 = [sb([128, 2, 6], F32, "st6") for _ in range(2)]
    mv = [sb([128, 4], F32, "mv") for _ in range(2)]
    pgu = [ps([128, 512], F32, "pgu") for _ in range(4)]
    py = [ps([128, 512], F32, "py") for _ in range(4)]
    cn = [0, 0, 0]
    blocks = [(i * BLK, BLK, False) for i in range(T // BLK)] + [(T, NS, True)]
    for bi, (t0, nb, smp) in enumerate(blocks):
        xb = x1T[bi % 2]
        S.dma("sp", xb, xb[:, :, 0:nb], X["x1T"], X["x1T"].t.ap().rearrange("(k p) t -> p k t", p=128)[:, :, t0:t0 + nb])
        H = hT[0]
        for fc in range(NFC):
            pg_ = pgu[cn[0] % 4]
            pu_ = pgu[(cn[0] + 1) % 4]
            cn[0] += 2

            def f_gu(e, pg_=pg_, pu_=pu_, fc=fc, xb=xb, nb=nb):
                ins = None
                for kc in range(8):
                    e.matmul(pg_[:, 0:nb], lhsT=wg[:, kc, fc * 128:(fc + 1) * 128], rhs=xb[:, kc, 0:nb], start=(kc == 0), stop=(kc == 7))
                for kc in range(8):
                    ins = e.matmul(pu_[:, 0:nb], lhsT=wu[:, kc, fc * 128:(fc + 1) * 128], rhs=xb[:, kc, 0:nb], start=(kc == 0), stop=(kc == 7))
                return ins
            S.op("pe", f_gu, reads=[wg, wu, xb], writes=[pg_, pu_])
            sg_ = sgt[fc % 2]
            S.op("act", lambda e, pg_=pg_, sg_=sg_, nb=nb: e.activation(out=sg_[:, 0:nb], in_=pg_[:, 0:nb], func=AF.Silu),
                 reads=[pg_], writes=[sg_])
            S.op("dve", lambda e, pu_=pu_, sg_=sg_, fc=fc, nb=nb, H=H: e.tensor_tensor(out=H[:, fc, 0:nb], in0=sg_[:, 0:nb], in1=pu_[:, 0:nb],
                                                                                    op=ALU.mult), reads=[sg_, pu_], writes=[H])
        for ms in range((nb + 127) // 128):
            n = min(128, nb - ms * 128)
            r0 = t0 + ms * 128
            xr = x1[cn[1] % 2]
            s6 = st6[cn[1] % 2]
            m_ = mv[cn[1] % 2]
            cn[1] += 1
            S.dma("sp", xr, xr[0:n, :], X["x1"], X["x1"].t.ap()[r0:r0 + n, :])
            for hf in range(2):
                pyy = py[cn[2] % 4]
                cn[2] += 1

                def f_d(e, pyy=pyy, hf=hf, ms=ms, n=n, H=H):
                    ins = None
                    for fc in range(NFC):
                        ins = e.matmul(pyy[0:n, :], lhsT=H[:, fc, ms * 128:ms * 128 + n], rhs=wd[:, fc, hf * 512:(hf + 1) * 512],
                                       start=(fc == 0), stop=(fc == NFC - 1))
                    return ins
                S.op("pe", f_d, reads=[H, wd], writes=[pyy])
                S.op("dve", lambda e, pyy=pyy, hf=hf, n=n, xr=xr: e.scalar_tensor_tensor(
                    out=xr[0:n, hf * 512:(hf + 1) * 512], in0=xr[0:n, hf * 512:(hf + 1) * 512], scalar=ALPHA, in1=pyy[0:n, :],
                    op0=ALU.mult, op1=ALU.add), reads=[xr, pyy], writes=[xr])
                S.op("dve", lambda e, hf=hf, n=n, xr=xr, s6=s6: e.bn_stats(out=s6[0:n, hf, :], in_=xr[0:n, hf * 512:(hf + 1) * 512]),
                     reads=[xr], writes=[s6])
            S.op("dve", lambda e, n=n, s6=s6, m_=m_: e.bn_aggr(out=m_[0:n, 0:2], in_=s6[0:n, :, :].rearrange("p a b -> p (a b)")),
                 reads=[s6], writes=[m_])
            S.op("act", lambda e, n=n, m_=m_: e.activation(out=m_[0:n, 2:3], in_=m_[0:n, 1:2], func=AF.Sqrt, bias=eps[0:n, 0:1], scale=1.0),
                 reads=[m_, eps], writes=[m_])
            S.op("dve", lambda e, n=n, m_=m_: e.reciprocal(out=m_[0:n, 3:4], in_=m_[0:n, 2:3]), reads=[m_], writes=[m_])
            S.op("dve", lambda e, n=n, m_=m_, xr=xr: e.tensor_scalar(out=xr[0:n, :], in0=xr[0:n, :], scalar1=m_[0:n, 0:1], scalar2=m_[0:n, 3:4],
                                                                     op0=ALU.subtract, op1=ALU.mult), reads=[xr, m_], writes=[xr])
            S.op("pool", lambda e, n=n, xr=xr: e.tensor_tensor(out=xr[0:n, :], in0=xr[0:n, :], in1=bc[0:n, 0, :], op=ALU.mult),
                 reads=[xr, bc], writes=[xr])
            S.op("pool", lambda e, n=n, xr=xr: e.tensor_tensor(out=xr[0:n, :], in0=xr[0:n, :], in1=bc[0:n, 1, :], op=ALU.add),
                 reads=[xr, bc], writes=[xr])
            yo = O["ys"] if smp else O["yp"]
            yr = 0 if smp else r0
            S.dma("pool", yo, yo.t.ap()[yr:yr + n, :], xr, xr[0:n, :])


def host_consts(T, PAST, NS):
    pos = np.concatenate([np.arange(T), PAST + np.arange(NS)]).astype(np.float64)
    inv = 10000.0 ** (-np.arange(32, dtype=np.float64) / 32)
    ang = pos[:, None] * inv[None, :]
    w = np.arange(896)
    bk = t5_bucket_np(127 - w)
    oh = (bk[None, :] == np.arange(32)[:, None]).astype(np.float32)
    gam = 1.0 - 2.0 ** (-5.0 - np.arange(8, dtype=np.float64))
    m = np.arange(128)
    diff = m[None, :] - m[:, None]
    DT = np.where(diff[:, None, :] >= 0, gam[None, :, None] ** np.maximum(diff[:, None, :], 0), 0.0) / 8.0
    hh = (np.arange(4)[None, :] * 2 + (np.arange(128)[:, None] // 64))
    Gq = gam[hh][:, :, None] ** (m[None, None, :] + 1.0)
    Gk = np.zeros((128, 2, 8))
    Gk[:, 0, :] = gam[None, :] ** (127.0 - m[:, None]) / 8.0
    Gk[:NS, 1, :] = gam[None, :] ** (NS - 1.0 - m[:NS, None]) / 8.0
    gC = np.zeros((128, 2, 8, 64))
    gC[:, 0] = (gam ** 128.0)[None, :, None]
    gC[:, 1] = (gam ** float(NS))[None, :, None]
    f32 = lambda a: np.ascontiguousarray(a, dtype=np.float32)
    return {"c_DT": f32(DT), "c_Gq": f32(Gq), "c_Gk": f32(Gk), "c_gC": f32(gC), "c_ident": np.eye(128, dtype=np.float32), "c_anti": np.ascontiguousarray(np.eye(128, dtype=np.float32)[::-1]), "c_oh": oh, "c_cos": np.cos(ang).astype(np.float32),
            "c_sin": np.sin(ang).astype(np.float32)}


def make_in_maps(inp, T, PAST, NS, ncores):
    cst = host_consts(T, PAST, NS)
    f = lambda a: np.ascontiguousarray(a, dtype=np.float32)
    maps = []
    for b in range(ncores):
        m = {"xp": f(inp["x_prompt"][b]), "xs": f(inp["x_sample"][b]),
             "ck": f(inp["cache_k"][0, b].reshape(PAST, 512)), "cv": f(inp["cache_v"][0, b].reshape(PAST, 512)),
             "cik": f(inp["cache_idx_k"][0, b]), "sr": f(inp["state_ret"][0, b].reshape(512, 128)),
             "w_in": f(inp["w_in"][0]), "idx_g": f(inp["idx_k_norm_g"]), "idx_b": f(inp["idx_k_norm_b"]),
             "t5": f(inp["t5_bias"]), "gn_g": f(inp["ret_gn_g"]), "w_pa": f(inp["w_pa"][0]), "w_pr": f(inp["w_pr"][0]),
             "w_o": f(inp["w_o"][0]), "ln1_g": f(inp["ln1_g"]), "ln1_b": f(inp["ln1_b"]), "w_gate": f(inp["w_gate"][0]),
             "w_up": f(inp["w_up"][0]), "w_down": f(inp["w_down"][0]), "ln2_g": f(inp["ln2_g"]), "ln2_b": f(inp["ln2_b"])}
        m.update(cst)
        maps.append(m)
    return maps


def assemble(results, T, NS):
    B = len(results)
    st = lambda k: np.stack([np.asarray(r[k], dtype=np.float32) for r in results])
    yp = st("yp")
    ys = st("ys")
    kp = st("kp").reshape(1, B, T, 8, 64)
    vp = st("vp").reshape(1, B, T, 8, 64)
    ikp = st("ikp").reshape(1, B, T, 64)
    stp = st("stp").reshape(1, B, 8, 64, 128)
    ks = st("ks").reshape(1, B, NS, 8, 64)
    vs = st("vs").reshape(1, B, NS, 8, 64)
    iks = st("iks").reshape(1, B, NS, 64)
    sts = st("sts").reshape(1, B, 8, 64, 128)
    return (yp, ys, kp, vp, ikp, stp, ks, vs, iks, sts)


def kernel(**inputs):
    T, PAST, NS = 8192, 4096, 32
    nc = build_program(T, PAST, NS)
    maps = make_in_maps(inputs, T, PAST, NS, 8)
    res = run_bass_kernel_spmd(nc, maps, core_ids=list(range(8)))
    return assemble(res.results, T, NS)
```

```python
import os
import numpy as np
from contextlib import ExitStack
SKIP = set(os.environ.get('KSKIP', '').split(','))
CUT = int(os.environ.get('KCUT', '99'))
import concourse.bass as bass
import concourse.mybir as mybir
from concourse.bass_utils import run_bass_kernel_spmd

F32 = mybir.dt.float32
BF16 = mybir.dt.bfloat16
ALU = mybir.AluOpType
AF = mybir.ActivationFunctionType
AX = mybir.AxisListType


class Buf:
    def __init__(self, t, name, dram=False):
        self.t = t
        self.name = name
        self.dram = dram
        self.w = {}
        self.r = {}
        self.si = None
        self.so = None
        self.ci = 0
        self.co = 0

    def __getitem__(self, k):
        return self.t[k]


class Sched:
    ENGS = ("pe", "act", "dve", "pool", "sp")
    EPOCH = 30000

    def __init__(self, nc, stack):
        self.nc = nc
        self.stack = stack
        self.sems = []
        self.ops = {e: [] for e in self.ENGS}
        self.cnt = {e: 0 for e in self.ENGS}
        self.own = {e: set() for e in self.ENGS}
        self.cur = {}
        self.seen = {e: {} for e in self.ENGS}
        self.latest = {}
        for _ in range(int(os.environ.get('KDUMMY', '0'))):
            self.new_sem('dummy')
        for e in self.ENGS:
            self._new_epoch(e)
        self.nops = 0

    def new_sem(self, name):
        h = self.stack.enter_context(self.nc.semaphore(f"{name}_{len(self.sems)}"))
        self.sems.append(h)
        return len(self.sems) - 1

    def _new_epoch(self, e):
        k = self.new_sem("e" + e)
        self.cur[e] = k
        self.own[e].add(k)
        self.cnt[e] = 0

    def _deps(self, reads, writes, skip=None):
        d = {}
        for b in reads:
            for k, v in b.w.items():
                if d.get(k, 0) < v:
                    d[k] = v
        for b in writes:
            if b.dram:
                continue
            for dic in (b.w, b.r):
                for k, v in dic.items():
                    if k == skip:
                        continue
                    if d.get(k, 0) < v:
                        d[k] = v
        return d

    def _waits(self, eng, d):
        seen = self.seen[eng]
        out = []
        for k, v in d.items():
            if eng == "pe" and k in self.own["pe"]:
                continue
            if seen.get(k, 0) >= v:
                continue
            seen[k] = v
            out.append((k, v))
        return out

    def op(self, eng, fn, reads=(), writes=()):
        d = self._deps(reads, writes)
        waits = self._waits(eng, d)
        if self.cnt[eng] >= self.EPOCH:
            self._new_epoch(eng)
        self.cnt[eng] += 1
        k, v = self.cur[eng], self.cnt[eng]
        self.latest[k] = v
        self.ops[eng].append((waits, fn, k, 1))
        self.nops += 1
        for b in reads:
            if not b.dram:
                b.r[k] = v
        for b in writes:
            if b.dram:
                b.w[k] = v
            else:
                b.w = {k: v}
                b.r = {}

    def dma(self, q, ob, oap, ib, iap, **kw):
        if not ob.dram:
            if ob.si is None or ob.ci >= 48000:
                ob.si, ob.ci = self.dma_sem(fresh=(q == "pool"))
            skip = ob.si
        else:
            if ib.so is None or ib.co >= 48000:
                ib.so, ib.co = self.dma_sem(fresh=(q == "pool"))
            skip = None
        d = self._deps([ib], [ob], skip=skip)
        waits = self._waits(q, d)
        if not ob.dram:
            ob.ci += 16
            k, v = ob.si, ob.ci
        else:
            ib.co += 16
            k, v = ib.so, ib.co
        self.latest[k] = v
        self.ops[q].append((waits, (lambda e: e.dma_start(out=oap, in_=iap, **kw)), k, 16))
        self.nops += 1
        if not ib.dram:
            ib.r[k] = v
        if ob.dram:
            ob.w[k] = v
        else:
            if ob.si in ob.w and len(ob.w) == 1:
                ob.w[k] = v
            else:
                ob.w = {k: v}
                ob.r = {}

    def dma_sem(self, fresh=False):
        fr = self.__dict__.setdefault("free_dma", [])
        while fr and not fresh:
            k = fr.pop()
            if self.latest.get(k, 0) < 40000:
                self.__dict__.setdefault("phase_dma", []).append(k)
                return k, self.latest.get(k, 0)
        k = self.new_sem("d")
        if not fresh:
            self.__dict__.setdefault("phase_dma", []).append(k)
        return k, 0

    def end_phase(self):
        self.__dict__.setdefault("free_dma", []).extend(self.__dict__.get("phase_dma", []))
        self.phase_dma = []

    def barrier(self):
        for e in self.ENGS:
            waits = self._waits(e, dict(self.latest))
            if waits:
                self.ops[e].append((waits, None, None, 0))

    def emit(self):
        nc = self.nc
        sems = self.sems
        engmap = {"pe": "tensor", "act": "scalar", "dve": "vector", "pool": "gpsimd", "sp": "sync"}
        with nc.Block() as block:
            for e in self.ENGS:
                lst = self.ops[e]

                def body(eng, lst=lst):
                    for waits, fn, k, inc in lst:
                        for wk, wv in waits:
                            eng.wait_ge(sems[wk], wv)
                        if fn is not None:
                            ins = fn(eng)
                            ins.then_inc(sems[k], inc)

                getattr(block, engmap[e])(body)
        self.ops = {e: [] for e in self.ENGS}


D = 1024
NH = 8
DH = 64
NHI = 4
DI = 64
NHR = 8
DKR = 64
DVR = 128
DFF = 2816
DIN = 6980
C_QA, C_KA, C_VA, C_QI, C_KI, C_WI, C_QR, C_KR, C_VR, C_GR, C_GA, C_GRR = (
    0, 512, 1024, 1536, 1792, 1856, 1860, 2372, 2884, 3908, 4932, 5956)
P1COLS = 3908
ALPHA = 2.0 ** 0.25
LN_EPS = 1e-5
GN_EPS = 1e-6
TOPK = 256
NEG = -30000.0


def dap(t, offset, dims):
    return bass.AP(tensor=t, offset=offset, ap=[list(d) for d in dims])


class Ctx:
    pass


def build_program(T, PAST, NS=32, phases=(1, 2, 3, 4), dbg=False):
    nc = bass.Bass("TRN2", target_bir_lowering=False)
    LS = PAST + NS
    NT = T // 128
    g = Ctx()
    g.nc = nc
    di = lambda n, s, dt=F32: nc.dram_tensor(n, s, dt, kind="ExternalInput")
    do = lambda n, s, dt=F32: nc.dram_tensor(n, s, dt, kind="ExternalOutput")
    dsx = lambda n, s, dt=BF16: nc.dram_tensor(n, s, dt, kind=("ExternalOutput" if dbg else "Internal"))
    I = {}
    for n, s in [("xp", [T, D]), ("xs", [NS, D]), ("ck", [PAST, 512]), ("cv", [PAST, 512]), ("cik", [PAST, 64]),
                 ("sr", [NHR * DKR, DVR]), ("w_in", [D, DIN]), ("idx_g", [1, 64]), ("idx_b", [1, 64]),
                 ("t5", [32, 8]), ("gn_g", [1, 1024]), ("w_pa", [512, D]), ("w_pr", [1024, D]), ("w_o", [D, D]),
                 ("ln1_g", [1, D]), ("ln1_b", [1, D]), ("w_gate", [D, DFF]), ("w_up", [D, DFF]), ("w_down", [DFF, D]),
                 ("ln2_g", [1, D]), ("ln2_b", [1, D]),
                 ("c_ident", [128, 128]), ("c_anti", [128, 128]), ("c_oh", [32, 896]), ("c_DT", [128, 8, 128]), ("c_Gq", [128, 4, 128]), ("c_Gk", [128, 2, 8]), ("c_gC", [128, 2, 8, 64]), ("c_cos", [T + NS, 32]), ("c_sin", [T + NS, 32])]:
        I[n] = Buf(di(n, s), n, True)
    O = {}
    for n, s in [("yp", [T, D]), ("ys", [NS, D]), ("kp", [T, 512]), ("vp", [T, 512]), ("ikp", [T, 64]),
                 ("stp", [NHR * DKR, DVR]), ("ks", [NS, 512]), ("vs", [NS, 512]), ("iks", [NS, 64]),
                 ("sts", [NHR * DKR, DVR])]:
        O[n] = Buf(do(n, s), n, True)
    X = {}
    TT = T + NS
    for n, s, dt in [("xT", [D, TT], BF16),
                     ("qaT", [128, 4, TT], BF16), ("qiT", [128, 2, TT], BF16), ("wi", [TT, 4], F32),
                     ("qrT", [128, 4, TT], BF16), ("krT", [128, 4, TT], BF16), ("kr", [TT, 512], BF16),
                     ("vr", [TT, 1024], BF16),
                     ("kaT_p", [128, 4, T], BF16), ("v_p", [T, 520], BF16), ("kiT_p", [64, T], BF16),
                     ("kaT_s", [128, 4, LS], BF16), ("v_s", [LS, 520], BF16), ("bvec", [8, 1024], BF16), ("kiT_s", [64, LS], BF16),
                     ("a", [TT, 512], F32), ("x1", [TT, D], F32), ("x1T", [D, TT], BF16)]:
        X[n] = Buf(dsx(n, s, dt), n, True)
    g.I, g.O, g.X = I, O, X
    g.T, g.PAST, g.NS, g.LS, g.NT, g.TT = T, PAST, NS, LS, NT, TT

    with ExitStack() as outer:
        S = Sched(nc, outer)
        g.S = S
        if 1 in phases:
            with ExitStack() as ph:
                phase1(g, ph)
                print('SBUF remaining after phase1:', nc.sbuf_bytes_remaining, flush=True)
                S.barrier()
                S.emit()
                S.end_phase()
        if 2 in phases:
            with ExitStack() as ph:
                phaseA(g, ph)
                print('SBUF remaining after phaseA:', nc.sbuf_bytes_remaining, flush=True)
                S.barrier()
                S.emit()
                S.end_phase()
        if 3 in phases:
            with ExitStack() as ph:
                phaseF1(g, ph)
                print('SBUF remaining after phaseF1:', nc.sbuf_bytes_remaining, flush=True)
                S.barrier()
                S.emit()
                S.end_phase()
        if 4 in phases:
            with ExitStack() as ph:
                phaseF2(g, ph)
                print('SBUF remaining after phaseF2:', nc.sbuf_bytes_remaining, flush=True)
                S.barrier()
                S.emit()
                S.end_phase()
        S.barrier()
        S.emit()
        global LAST_NOPS
        LAST_NOPS = (S.nops, len(S.sems), {e: len(S.own[e]) for e in S.ENGS})
    return nc


def mk_alloc(g, ph):
    nc = g.nc
    if not hasattr(g, 'alloc_cnt'):
        g.alloc_cnt = [0]
    cnt = g.alloc_cnt

    def sb(shape, dt, name):
        cnt[0] += 1
        return Buf(ph.enter_context(nc.sbuf_tensor(f"{name}_{cnt[0]}", shape, dt)), name)

    def ps(shape, dt, name):
        cnt[0] += 1
        return Buf(ph.enter_context(nc.psum_tensor(f"{name}_{cnt[0]}", shape, dt)), name)

    return sb, ps


def phase1(g, ph):
    S, I, O, X = g.S, g.I, g.O, g.X
    T, PAST, NS, NT = g.T, g.PAST, g.NS, g.NT
    sb, ps = mk_alloc(g, ph)
    w16 = sb([128, 8, P1COLS], BF16, "w16")
    for kc in range(8):
        S.dma("pool", w16, w16[:, kc, :], I["w_in"], I["w_in"].t.ap()[kc * 128:(kc + 1) * 128, 0:P1COLS])
    id16 = sb([128, 128], BF16, "id16")
    S.dma("pool", id16, id16[:, :], I["c_ident"], I["c_ident"].t.ap()[:, :])
    gb = sb([128, 2, 64], F32, "gb")
    S.dma("sp", gb, gb[:, 0, :], I["idx_g"], dap(I["idx_g"].t, 0, [[0, 128], [1, 64]]))
    S.dma("sp", gb, gb[:, 1, :], I["idx_b"], dap(I["idx_b"].t, 0, [[0, 128], [1, 64]]))
    eps = sb([128, 1], F32, "eps")
    S.op("dve", lambda e: e.memset(eps[:, :], LN_EPS), writes=[eps])
    NB = 3
    x16 = [sb([128, D], BF16, "x16") for _ in range(NB)]
    xT16 = [sb([128, 8, 128], BF16, "xT16") for _ in range(NB)]
    h32 = [sb([128, P1COLS], F32, "h32") for _ in range(NB)]
    q16 = [sb([128, 2368], BF16, "q16") for _ in range(NB)]
    fT16 = [sb([128, 19, 128], BF16, "fT16") for _ in range(NB)]
    v16 = [sb([128, 520], BF16, "v16") for _ in range(NB)]
    for vb in v16:
        S.op("pool", lambda e, vb=vb: e.memset(vb[:, :], 1.0), writes=[vb])
    vr16 = [sb([128, 1024], BF16, "vr16") for _ in range(NB)]
    cs = [sb([128, 2, 32], F32, "cs") for _ in range(NB)]
    rt = [sb([128, 4, 16, 32], F32, "rt") for _ in range(NB)]
    st6 = [sb([128, 6], F32, "st6") for _ in range(NB)]
    mv = [sb([128, 4], F32, "mv") for _ in range(NB)]
    kin = [sb([128, 64], F32, "kin") for _ in range(NB)]
    wi = [sb([128, 4], F32, "wi") for _ in range(NB)]
    pxT = [ps([128, 1024], BF16, "pxT") for _ in range(1)]
    pmm = [ps([128, 512], F32, "pmm") for _ in range(4)]
    pfT = [ps([128, 1024], BF16, "pfT") for _ in range(3)]
    mmi = 0

    def p1_loads(t2):
        smp2 = t2 == NT
        n2 = NS if smp2 else 128
        b2 = t2 % NB
        xsrc2 = I["xs"] if smp2 else I["xp"]
        xrow2 = 0 if smp2 else t2 * 128
        S.dma("pool", x16[b2], x16[b2][0:n2, :], xsrc2, xsrc2.t.ap()[xrow2:xrow2 + n2, :])
        S.dma("sp", cs[b2], cs[b2][0:n2, 0, :], I["c_cos"], I["c_cos"].t.ap()[t2 * 128:t2 * 128 + n2, :])
        S.dma("sp", cs[b2], cs[b2][0:n2, 1, :], I["c_sin"], I["c_sin"].t.ap()[t2 * 128:t2 * 128 + n2, :])

    for tt in range(NT + 1):
        smp = tt == NT
        n = NS if smp else 128
        b = tt % NB
        tok0 = tt * 128
        xsrc = I["xs"] if smp else I["xp"]
        xrow = 0 if smp else tok0
        for t2 in ([0, 1, 2] if tt == 0 else [tt + 2]):
            if t2 <= NT:
                p1_loads(t2)
        px = pxT[0]

        def f_xt(e, b=b, n=n, px=px):
            ins = None
            for kc in range(8):
                ins = e.transpose(px[:, kc * 128:kc * 128 + n], x16[b][0:n, kc * 128:(kc + 1) * 128], id16[0:n, 0:n])
            return ins
        S.op("pe", f_xt, reads=[x16[b], id16], writes=[px])
        S.op("dve", lambda e, b=b, n=n, px=px: e.tensor_copy(
            out=xT16[b][:, :, 0:n], in_=px[:, :].rearrange("p (k t) -> p k t", k=8)[:, :, 0:n]),
            reads=[px], writes=[xT16[b]])
        S.dma("sp", X["xT"], X["xT"].t.ap().rearrange("(k p) t -> p k t", p=128)[:, :, tok0:tok0 + n],
              xT16[b], xT16[b][:, :, 0:n])
        nblk = (P1COLS + 511) // 512
        for cb in range(nblk):
            c0 = cb * 512
            cw = min(512, P1COLS - c0)
            pm = pmm[mmi % 4]
            mmi += 1

            def f_mm(e, b=b, n=n, pm=pm, c0=c0, cw=cw):
                ins = None
                for kc in range(8):
                    ins = e.matmul(pm[0:n, 0:cw], lhsT=xT16[b][:, kc, 0:n], rhs=w16[:, kc, c0:c0 + cw],
                                   start=(kc == 0), stop=(kc == 7))
                return ins
            S.op("pe", f_mm, reads=[xT16[b], w16], writes=[pm])
            if cb % 2 == 0:
                S.op("act", lambda e, b=b, n=n, pm=pm, c0=c0, cw=cw: e.activation(
                    out=h32[b][0:n, c0:c0 + cw], in_=pm[0:n, 0:cw], func=AF.Copy), reads=[pm], writes=[h32[b]])
            else:
                S.op("dve", lambda e, b=b, n=n, pm=pm, c0=c0, cw=cw: e.tensor_copy(
                    out=h32[b][0:n, c0:c0 + cw], in_=pm[0:n, 0:cw]), reads=[pm], writes=[h32[b]])
        H = h32[b]
        ko, vo, iko = (O["ks"], O["vs"], O["iks"]) if smp else (O["kp"], O["vp"], O["ikp"])
        orow = 0 if smp else tok0
        S.dma("sp", ko, ko.t.ap()[orow:orow + n, :], H, H[0:n, C_KA:C_KA + 512])
        S.dma("sp", vo, vo.t.ap()[orow:orow + n, :], H, H[0:n, C_VA:C_VA + 512])
        S.op("dve", lambda e, b=b, n=n, H=H: e.bn_stats(out=st6[b][0:n, :], in_=H[0:n, C_KI:C_KI + 64]),
             reads=[H], writes=[st6[b]])
        S.op("dve", lambda e, b=b, n=n: e.bn_aggr(out=mv[b][0:n, 0:2], in_=st6[b][0:n, :]),
             reads=[st6[b]], writes=[mv[b]])
        S.op("act", lambda e, b=b, n=n: e.activation(out=mv[b][0:n, 2:3], in_=mv[b][0:n, 1:2], func=AF.Sqrt,
                                                    bias=eps[0:n, 0:1], scale=1.0), reads=[mv[b], eps], writes=[mv[b]])
        S.op("dve", lambda e, b=b, n=n: e.reciprocal(out=mv[b][0:n, 3:4], in_=mv[b][0:n, 2:3]),
             reads=[mv[b]], writes=[mv[b]])
        S.op("dve", lambda e, b=b, n=n, H=H: e.tensor_scalar(
            out=kin[b][0:n, :], in0=H[0:n, C_KI:C_KI + 64], scalar1=mv[b][0:n, 0:1], scalar2=mv[b][0:n, 3:4],
            op0=ALU.subtract, op1=ALU.mult), reads=[H, mv[b]], writes=[kin[b]])
        S.op("dve", lambda e, b=b, n=n: e.tensor_tensor(out=kin[b][0:n, :], in0=kin[b][0:n, :], in1=gb[0:n, 0, :],
                                                       op=ALU.mult), reads=[kin[b], gb], writes=[kin[b]])
        S.op("dve", lambda e, b=b, n=n: e.tensor_tensor(out=kin[b][0:n, :], in0=kin[b][0:n, :], in1=gb[0:n, 1, :],
                                                       op=ALU.add), reads=[kin[b], gb], writes=[kin[b]])
        S.dma("sp", iko, iko.t.ap()[orow:orow + n, :], kin[b], kin[b][0:n, :])
        S.op("pool", lambda e, b=b, n=n, H=H: e.tensor_scalar(out=wi[b][0:n, :], in0=H[0:n, C_WI:C_WI + 4],
                                                             scalar1=0.0625, scalar2=None, op0=ALU.mult),
             reads=[H], writes=[wi[b]])
        S.dma("sp", X["wi"], X["wi"].t.ap()[tok0:tok0 + n, :], wi[b], wi[b][0:n, :])
        Q = q16[b]
        S.op("pool", lambda e, n=n, H=H, Q=Q: e.tensor_copy(out=Q[0:n, 0:1024], in_=H[0:n, C_QA:C_QA + 1024]),
             reads=[H], writes=[Q])
        S.op("pool", lambda e, n=n, H=H, Q=Q: e.tensor_copy(out=Q[0:n, 1024:1280], in_=H[0:n, C_QI:C_QI + 256]),
             reads=[H], writes=[Q])
        S.op("pool", lambda e, b=b, n=n, Q=Q: e.tensor_copy(out=Q[0:n, 1280:1344], in_=kin[b][0:n, :]),
             reads=[kin[b]], writes=[Q])
        S.op("act", lambda e, b=b, n=n, H=H: e.activation(out=v16[b][0:n, :].rearrange("p (h d) -> p h d", d=65)[:, :, 0:64], in_=H[0:n, C_VA:C_VA + 512].rearrange("p (h d) -> p h d", d=64), func=AF.Copy),
             reads=[H], writes=[v16[b]])
        S.op("act", lambda e, b=b, n=n, H=H: e.activation(out=vr16[b][0:n, :], in_=H[0:n, C_VR:C_VR + 1024], func=AF.Copy),
             reads=[H], writes=[vr16[b]])
        R = rt[b]
        xv = H[0:n, C_QR:C_QR + 1024].rearrange("p (h d) -> p h d", h=16)
        x1, x2 = xv[:, :, 0:32], xv[:, :, 32:64]
        cb_ = cs[b][0:n, 0:1, :].to_broadcast([n, 16, 32])
        sb_ = cs[b][0:n, 1:2, :].to_broadcast([n, 16, 32])
        ov = Q[0:n, 1344:2368].rearrange("p (h d) -> p h d", h=16)
        for j, (xa, tb) in enumerate([(x1, cb_), (x2, sb_), (x1, sb_), (x2, cb_)]):
            eng = "dve" if j % 2 == 0 else "pool"
            S.op(eng, lambda e, j=j, xa=xa, tb=tb, R=R, n=n: e.tensor_tensor(out=R[0:n, j, :, :], in0=xa, in1=tb, op=ALU.mult),
                 reads=[H, cs[b]], writes=[R])
        S.op("dve", lambda e, R=R, n=n, ov=ov: e.tensor_tensor(out=ov[:, :, 0:32], in0=R[0:n, 0, :, :], in1=R[0:n, 1, :, :],
                                                                op=ALU.subtract), reads=[R], writes=[Q])
        S.op("pool", lambda e, R=R, n=n, ov=ov: e.tensor_tensor(out=ov[:, :, 32:64], in0=R[0:n, 2, :, :], in1=R[0:n, 3, :, :],
                                                                 op=ALU.add), reads=[R], writes=[Q])
        vdst = X["v_s"] if smp else X["v_p"]
        vrow = PAST if smp else tok0
        S.dma("sp", vdst, vdst.t.ap()[vrow:vrow + n, :], v16[b], v16[b][0:n, :])
        S.dma("sp", X["vr"], X["vr"].t.ap()[tok0:tok0 + n, :], vr16[b], vr16[b][0:n, :])
        S.dma("sp", X["kr"], X["kr"].t.ap()[tok0:tok0 + n, :], Q, Q[0:n, 1856:2368])
        srcs = [(i * 128, 128) for i in range(10)] + [(1344 + i * 128, 128) for i in range(8)] + [(1280, 64)]
        F = fT16[b]
        for gi in range(3):
            pf = pfT[gi]
            sl = list(range(gi * 8, min(19, gi * 8 + 8)))

            def f_ft(e, sl=sl, pf=pf, n=n, Q=Q):
                ins = None
                for j, si in enumerate(sl):
                    c0, cw = srcs[si]
                    ins = e.transpose(pf[0:cw, j * 128:j * 128 + n], Q[0:n, c0:c0 + cw], id16[0:n, 0:n])
                return ins
            S.op("pe", f_ft, reads=[Q, id16], writes=[pf])
            nfull = len(sl) if gi < 2 else 2
            pv = pf[:, 0:nfull * 128].rearrange("p (k t) -> p k t", t=128)[:, :, 0:n]
            if gi == 1:
                S.op("act", lambda e, sl=sl, pv=pv, n=n, F=F, nfull=nfull: e.activation(
                    out=F[:, sl[0]:sl[0] + nfull, 0:n], in_=pv, func=AF.Copy), reads=[pf], writes=[F])
            else:
                S.op("dve", lambda e, sl=sl, pv=pv, n=n, F=F, nfull=nfull: e.tensor_copy(
                    out=F[:, sl[0]:sl[0] + nfull, 0:n], in_=pv), reads=[pf], writes=[F])
            if gi == 2:
                S.op("dve", lambda e, pf=pf, n=n, F=F: e.tensor_copy(out=F[0:64, 18, 0:n], in_=pf[0:64, 256:256 + n]),
                     reads=[pf], writes=[F])
        kdst = X["kaT_s"] if smp else X["kaT_p"]
        kidst = X["kiT_s"] if smp else X["kiT_p"]
        kcol = PAST if smp else tok0
        S.dma("sp", X["qaT"], X["qaT"].t.ap()[:, :, tok0:tok0 + n], F, F[:, 0:4, 0:n])
        S.dma("sp", kdst, kdst.t.ap()[:, :, kcol:kcol + n], F, F[:, 4:8, 0:n])
        S.dma("sp", X["qiT"], X["qiT"].t.ap()[:, :, tok0:tok0 + n], F, F[:, 8:10, 0:n])
        S.dma("sp", kidst, kidst.t.ap()[:, kcol:kcol + n], F, F[0:64, 18, 0:n])
        S.dma("sp", X["qrT"], X["qrT"].t.ap()[:, :, tok0:tok0 + n], F, F[:, 10:14, 0:n])
        S.dma("sp", X["krT"], X["krT"].t.ap()[:, :, tok0:tok0 + n], F, F[:, 14:18, 0:n])

    ck16 = [sb([128, 576], BF16, "ck16") for _ in range(2)]
    cv16 = [sb([128, 520], BF16, "cv16") for _ in range(2)]
    cF = [sb([128, 5, 128], BF16, "cF") for _ in range(2)]
    for vb in cv16:
        S.op("pool", lambda e, vb=vb: e.memset(vb[:, :], 1.0), writes=[vb])
    for j in range(PAST // 128):
        b = j % 2
        r0 = j * 128
        S.dma("pool", ck16[b], ck16[b][:, 0:512], I["ck"], I["ck"].t.ap()[r0:r0 + 128, :])
        S.dma("pool", ck16[b], ck16[b][:, 512:576], I["cik"], I["cik"].t.ap()[r0:r0 + 128, :])
        S.dma("pool", cv16[b], cv16[b][:, :].rearrange("p (h d) -> p h d", d=65)[:, :, 0:64],
              I["cv"], I["cv"].t.ap()[r0:r0 + 128, :].rearrange("p (h d) -> p h d", d=64))
        S.dma("sp", X["v_s"], X["v_s"].t.ap()[r0:r0 + 128, :], cv16[b], cv16[b][:, :])
        pf = pfT[j % 3]

        def f_ct(e, b=b, pf=pf):
            ins = None
            for i in range(4):
                ins = e.transpose(pf[:, i * 128:(i + 1) * 128], ck16[b][:, i * 128:(i + 1) * 128], id16[:, :])
            ins = e.transpose(pf[0:64, 512:640], ck16[b][:, 512:576], id16[:, :])
            return ins
        S.op("pe", f_ct, reads=[ck16[b], id16], writes=[pf])
        S.op("dve", lambda e, b=b, pf=pf: e.tensor_copy(out=cF[b][:, 0:4, :], in_=pf[:, 0:512].rearrange("p (k t) -> p k t", t=128)),
             reads=[pf], writes=[cF[b]])
        S.op("act", lambda e, b=b, pf=pf: e.activation(out=cF[b][0:64, 4, :], in_=pf[0:64, 512:640], func=AF.Copy),
             reads=[pf], writes=[cF[b]])
        S.dma("sp", X["kaT_s"], X["kaT_s"].t.ap()[:, :, r0:r0 + 128], cF[b], cF[b][:, 0:4, :])
        S.dma("sp", X["kiT_s"], X["kiT_s"].t.ap()[:, r0:r0 + 128], cF[b], cF[b][0:64, 4, :])


def t5_bucket_np(rel):
    nb = 16
    max_exact = 8
    ret = np.where(rel > 0, nb, 0)
    n = np.abs(rel)
    nf = np.maximum(n, max_exact).astype(np.float32)
    large = max_exact + (np.log(nf / np.float32(max_exact)) / np.float32(np.log(1024 / max_exact)) * np.float32(nb - max_exact)).astype(np.int32)
    large = np.minimum(large, nb - 1)
    return ret + np.where(n < max_exact, n, large)


RLEN = 1024
LAST_NOPS = None


def phaseA(g, ph):
    S, I, O, X = g.S, g.I, g.O, g.X
    T, PAST, NS, NT, LS = g.T, g.PAST, g.NS, g.NT, g.LS
    sb, ps = mk_alloc(g, ph)
    NIT = int(os.environ.get("KNIT", "20"))
    LMAX = max(T, LS)
    NKT_MAX = (LMAX + 127) // 128
    QBMAX = 512
    id16 = sb([128, 128], BF16, "id16")
    anti16 = sb([128, 128], BF16, "anti16")
    S.dma("pool", id16, id16[:, :], I["c_ident"], I["c_ident"].t.ap()[:, :])
    S.dma("pool", anti16, anti16[:, :], I["c_anti"], I["c_anti"].t.ap()[:, :])
    pS = [ps([128, 512], F32, "pS") for _ in range(3)]
    pT = ps([128, 1024], BF16, "pT")
    pacc = [ps([128, 512], F32, "pacc") for _ in range(4)]
    t5sb = sb([32, 8], F32, "t5sb")
    oh = sb([32, 896], F32, "oh")
    S.dma("sp", t5sb, t5sb[:, :], I["t5"], I["t5"].t.ap()[:, :])
    S.dma("sp", oh, oh[:, :], I["c_oh"], I["c_oh"].t.ap()[:, :])
    bv16 = sb([8, 896], BF16, "bv16")
    for hf in range(2):
        pb = pS[hf]
        S.op("pe", lambda e, pb=pb, hf=hf: e.matmul(pb[0:8, 0:448], lhsT=t5sb[:, :], rhs=oh[:, hf * 448:(hf + 1) * 448],
                                                    start=True, stop=True), reads=[t5sb, oh], writes=[pb])
        S.op("act", lambda e, pb=pb, hf=hf: e.activation(out=bv16[:, hf * 448:(hf + 1) * 448], in_=pb[0:8, 0:448],
                                                        func=AF.Copy, scale=8.0), reads=[pb], writes=[bv16])
    S.dma("sp", X["bvec"], X["bvec"].t.ap()[:, 0:896], bv16, bv16[:, :])
    Mt = sb([128, 6, 8, 128], BF16, "Mt")
    for dl in range(6):
        S.dma("sp", Mt, Mt[:, dl, :, :], X["bvec"], dap(X["bvec"].t, 128 * dl, [[1, 128], [RLEN, 8], [1, 128]]))
    cfar = sb([128, 8], F32, "cfar")
    S.dma("sp", cfar, cfar[:, :], I["t5"], dap(I["t5"].t, 15 * 8, [[0, 128], [1, 8]]))
    cfar8 = sb([128, 8], F32, "cfar8")
    S.op("dve", lambda e: e.tensor_scalar(out=cfar8[:, :], in0=cfar[:, :], scalar1=8.0, scalar2=None, op0=ALU.mult),
         reads=[cfar], writes=[cfar8])
    Mfar = sb([128, 8, 128], BF16, "Mfar")
    S.op("dve", lambda e: e.tensor_copy(out=Mfar[:, :, :], in_=cfar8[:, :].unsqueeze(2).to_broadcast([128, 8, 128])),
         reads=[cfar8], writes=[Mfar])
    zero1 = sb([128, 1], F32, "zero1")
    S.op("dve", lambda e: e.memset(zero1[:, :], 0.0), writes=[zero1])
    pw2 = sb([128, NIT], F32, "pw2")
    for it in range(NIT):
        S.op("dve", lambda e, it=it: e.memset(pw2[:, it:it + 1], 0.5 ** it), writes=[pw2])
    acc = sb([128, LMAX], F32, "acc")
    nm16 = sb([128, LMAX], BF16, "nm16")
    negT2 = [sb([128, NKT_MAX, 256], BF16, "negT") for _ in range(2)]
    rh = [sb([128, 512], F32, "rh") for _ in range(4)]
    ki2 = [sb([128, 512], BF16, "ki2") for _ in range(2)]
    qi_t = [sb([128, 2, 128], BF16, "qi_t") for _ in range(2)]
    wi_t = [sb([128, 4], F32, "wi_t") for _ in range(2)]
    sc = [sb([128, 8], F32, "sc") for _ in range(2)]
    wv = [sb([128, NIT], F32, "wv") for _ in range(2)]
    qa_b = [sb([128, 4, QBMAX], BF16, "qa_b") for _ in range(2)]
    Kc = [sb([128, 2, 512], BF16, "Kc") for _ in range(3)]
    Vc = [sb([128, 4, 4, 65], BF16, "Vc") for _ in range(3)]
    PT = [sb([128, QBMAX], BF16, "PT") for _ in range(3)]
    a32 = [sb([128, 4, 512], F32, "a32") for _ in range(2)]
    rs = [sb([128, 4, 8], F32, "rs") for _ in range(2)]
    cnts = {"mm": 0, "rh": 0, "ki": 0, "kc": 0, "pt": 0, "qt": 0}

    def group(smp):
        if smp:
            QB, npq, nblk, tokbase = NS, NS, 1, T
            kaT, vS, kiT, Ltot = X["kaT_s"], X["v_s"], X["kiT_s"], LS
        else:
            QB, npq, nblk, tokbase = 256, 128, T // 256, 0
            kaT, vS, kiT, Ltot = X["kaT_p"], X["v_p"], X["kiT_p"], T
        nsub = QB // npq
        topk = min(TOPK, Ltot // 4)
        def genA(bi):
            q0 = tokbase + bi * QB
            nkeys = Ltot if smp else (bi + 1) * QB
            ktiles = [(k, min(128, nkeys - k)) for k in range(0, nkeys, 128)]
            nkt = len(ktiles)
            NT_ = negT2[bi % 2]
            for ms in range(nsub):
                qt = cnts["qt"] % 2
                cnts["qt"] += 1
                tq0 = q0 + ms * npq
                Lm = Ltot if smp else 128 * (nsub * bi + ms + 1)
                S.dma("sp", qi_t[qt], qi_t[qt][:, :, 0:npq], X["qiT"], X["qiT"].t.ap()[:, :, tq0:tq0 + npq])
                S.dma("sp", wi_t[qt], wi_t[qt][0:npq, :], X["wi"], X["wi"].t.ap()[tq0:tq0 + npq, :])
                for k0 in range(0, Lm, 512):
                    kn = min(512, Lm - k0)
                    kb = ki2[cnts["ki"] % 2]
                    cnts["ki"] += 1
                    S.dma("sp", kb, kb[0:64, 0:kn], kiT, kiT.t.ap()[:, k0:k0 + kn])
                    S.dma("sp", kb, kb[64:128, 0:kn], kiT, kiT.t.ap()[:, k0:k0 + kn])
                    for h in range(4):
                        pm = pS[cnts["mm"] % 3]
                        cnts["mm"] += 1
                        r = rh[cnts["rh"] % 4]
                        cnts["rh"] += 1
                        pb0 = (h % 2) * 64
                        S.op("pe", lambda e, pm=pm, qt=qt, h=h, pb0=pb0, kb=kb, kn=kn: e.matmul(
                            pm[0:npq, 0:kn], lhsT=qi_t[qt][pb0:pb0 + 64, h // 2, 0:npq], rhs=kb[pb0:pb0 + 64, 0:kn],
                            start=True, stop=True), reads=[qi_t[qt], kb], writes=[pm])
                        S.op("act", lambda e, pm=pm, r=r, kn=kn: e.activation(out=r[0:npq, 0:kn], in_=pm[0:npq, 0:kn], func=AF.Relu),
                             reads=[pm], writes=[r])
                        eng = "dve"
                        if h == 0:
                            S.op(eng, lambda e, r=r, qt=qt, k0=k0, kn=kn: e.tensor_scalar(
                                out=acc[0:npq, k0:k0 + kn], in0=r[0:npq, 0:kn], scalar1=wi_t[qt][0:npq, 0:1], scalar2=None,
                                op0=ALU.mult), reads=[r, wi_t[qt]], writes=[acc])
                        else:
                            S.op(eng, lambda e, r=r, qt=qt, k0=k0, kn=kn, h=h: e.scalar_tensor_tensor(
                                out=acc[0:npq, k0:k0 + kn], in0=r[0:npq, 0:kn], scalar=wi_t[qt][0:npq, h:h + 1],
                                in1=acc[0:npq, k0:k0 + kn], op0=ALU.mult, op1=ALU.add), reads=[r, wi_t[qt], acc], writes=[acc])
                s_ = sc[qt]
                w_ = wv[qt]
                S.op("dve", lambda e, s_=s_, Lm=Lm: e.tensor_reduce(out=s_[0:npq, 0:1], in_=acc[0:npq, 0:Lm], axis=AX.X, op=ALU.max,
                                                                    apply_absolute_value=True), reads=[acc], writes=[s_])
                if not smp:
                    S.op("dve", lambda e, Lm=Lm: e.memset(acc[0:64, Lm - 64:Lm], -1.0e30), reads=[s_], writes=[acc])
                S.op("dve", lambda e, s_=s_: e.tensor_scalar(out=s_[0:npq, 0:1], in0=s_[0:npq, 0:1], scalar1=1.0, scalar2=None,
                                                             op0=ALU.add), reads=[s_], writes=[s_])
                S.op("dve", lambda e, s_=s_: e.tensor_scalar(out=s_[0:npq, 1:2], in0=s_[0:npq, 0:1], scalar1=-1.0, scalar2=None,
                                                             op0=ALU.mult), reads=[s_], writes=[s_])
                S.op("dve", lambda e, s_=s_, w_=w_: e.tensor_scalar(out=w_[0:npq, :], in0=pw2[0:npq, :], scalar1=s_[0:npq, 0:1],
                                                                    scalar2=None, op0=ALU.mult), reads=[s_, pw2], writes=[w_])
                S.op("dve", lambda e, s_=s_: e.memset(s_[0:npq, 2:3], 0.0), writes=[s_])
                for it in range(NIT):
                    S.op("dve", lambda e, s_=s_, Lm=Lm: e.tensor_scalar(
                        out=nm16[0:npq, 0:Lm], in0=acc[0:npq, 0:Lm], scalar1=s_[0:npq, 2:3], scalar2=0.0, op0=ALU.is_ge,
                        op1=ALU.add, accum_out=s_[0:npq, 3:4]), reads=[acc, s_], writes=[nm16, s_])
                    S.op("dve", lambda e, s_=s_, w_=w_, it=it: e.tensor_scalar(
                        out=s_[0:npq, 4:5], in0=s_[0:npq, 3:4], scalar1=topk - 0.5, scalar2=w_[0:npq, it:it + 1],
                        op0=ALU.is_ge, op1=ALU.mult), reads=[s_, w_], writes=[s_])
                    wc = it + 1 if it < NIT - 1 else it
                    oc = 2 if it < NIT - 1 else 1
                    S.op("dve", lambda e, s_=s_, w_=w_, wc=wc, oc=oc: e.scalar_tensor_tensor(
                        out=s_[0:npq, oc:oc + 1], in0=s_[0:npq, 4:5], scalar=w_[0:npq, wc:wc + 1], in1=s_[0:npq, 2:3],
                        op0=ALU.subtract, op1=ALU.add), reads=[s_, w_], writes=[s_])
                S.op("dve", lambda e, s_=s_, Lm=Lm: e.tensor_scalar(
                    out=nm16[0:npq, 0:Lm], in0=acc[0:npq, 0:Lm], scalar1=s_[0:npq, 1:2], scalar2=NEG, op0=ALU.is_lt,
                    op1=ALU.mult), reads=[acc, s_], writes=[nm16])
                yield
                mt = [(k, w) for (k, w) in ktiles if k < Lm]
                for g0 in range(0, len(mt), 8):
                    grp = mt[g0:g0 + 8]

                    def f_tr(e, grp=grp):
                        ins = None
                        for j, (k, w) in enumerate(grp):
                            ins = e.transpose(pT[0:w, j * 128:j * 128 + npq], nm16[0:npq, k:k + w], id16[0:npq, 0:npq])
                        return ins
                    S.op("pe", f_tr, reads=[nm16, id16], writes=[pT])
                    full = [x for x in grp if x[1] == 128]
                    if full:
                        nf = len(full)
                        S.op("act", lambda e, g0=g0, nf=nf, ms=ms: e.activation(
                            out=NT_[:, g0:g0 + nf, ms * npq:(ms + 1) * npq],
                            in_=pT[:, 0:nf * 128].rearrange("p (k t) -> p k t", t=128)[:, :, 0:npq], func=AF.Copy),
                            reads=[pT], writes=[NT_])
                    if len(full) < len(grp):
                        j = len(grp) - 1
                        k, w = grp[j]
                        S.op("act", lambda e, g0=g0, j=j, w=w, ms=ms: e.activation(
                            out=NT_[0:w, g0 + j, ms * npq:(ms + 1) * npq], in_=pT[0:w, j * 128:j * 128 + npq], func=AF.Copy),
                            reads=[pT], writes=[NT_])
                if len(mt) < nkt:
                    S.op("pool", lambda e, a=len(mt), ms=ms, nkt=nkt: e.memset(NT_[:, a:nkt, ms * npq:(ms + 1) * npq], NEG), writes=[NT_])

        def genB(bi):
            q0 = tokbase + bi * QB
            nkeys = Ltot if smp else (bi + 1) * QB
            ktiles = [(k, min(128, nkeys - k)) for k in range(0, nkeys, 128)]
            nkt = len(ktiles)
            NT_ = negT2[bi % 2]
            pa = pacc[(bi % 2) * 2:(bi % 2) * 2 + 2]
            qb = qa_b[bi % 2]
            S.dma("sp", qb, qb[:, :, 0:QB], X["qaT"], X["qaT"].t.ap()[:, :, q0:q0 + QB])
            A32 = a32[bi % 2]
            for hg in range(2):
                pend = []
                for c0 in range(0, nkt, 4):
                    ch = ktiles[c0:c0 + 4]
                    kc = Kc[cnts["kc"] % 3]
                    vc = Vc[cnts["kc"] % 3]
                    cnts["kc"] += 1
                    kk0 = ch[0][0]
                    kkn = sum(w for _, w in ch)
                    S.dma("sp", kc, kc[:, :, 0:kkn], kaT, kaT.t.ap()[:, 2 * hg:2 * hg + 2, kk0:kk0 + kkn])
                    if all(w == 128 for _, w in ch):
                        S.dma("sp", vc, vc[:, 0:len(ch), :, :], vS,
                              vS.t.ap()[kk0:kk0 + kkn, hg * 260:(hg + 1) * 260].rearrange("(j p) (h d) -> p j h d", p=128, d=65))
                    else:
                        for j, (k, w) in enumerate(ch):
                            S.dma("sp", vc, vc[0:w, j, :, :], vS,
                                  vS.t.ap()[k:k + w, hg * 260:(hg + 1) * 260].rearrange("p (h d) -> p h d", d=65))
                    for hl in range(4):
                        h = hg * 4 + hl
                        pb0 = (h % 2) * 64
                        for j, (k, w) in enumerate(ch):
                            jt = c0 + j
                            pm = pS[cnts["mm"] % 3]
                            cnts["mm"] += 1
                            if smp:
                                deltas = [jt - PAST // 128]
                            else:
                                deltas = [jt - (nsub * bi + ms) for ms in range(nsub)]
                            allfar = all(dl <= -6 for dl in deltas)

                            def f_s(e, pm=pm, kc=kc, qb=qb, pb0=pb0, hl=hl, j=j, w=w, jt=jt, h=h, deltas=deltas, allfar=allfar):
                                e.matmul(pm[0:w, 0:QB], lhsT=kc[pb0:pb0 + 64, hl // 2, j * 128:j * 128 + w],
                                         rhs=qb[pb0:pb0 + 64, h // 2, 0:QB], start=True, stop=False)
                                ins = None
                                if not allfar:
                                    for ms, dl in enumerate(deltas):
                                        if dl > 0:
                                            continue
                                        rhs = Mfar[:, h, 0:npq] if dl <= -6 else Mt[:, -dl, h, 0:npq]
                                        e.matmul(pm[0:w, ms * npq:(ms + 1) * npq], lhsT=anti16[:, 0:w], rhs=rhs, start=False, stop=False)
                                ins = e.matmul(pm[0:w, 0:QB], lhsT=id16[0:w, 0:w], rhs=NT_[0:w, jt, 0:QB], start=False, stop=True)
                                return ins
                            S.op("pe", f_s, reads=[kc, qb, NT_, Mt, Mfar, id16, anti16], writes=[pm])
                            pt = PT[cnts["pt"] % 3]
                            cnts["pt"] += 1
                            bias_ap = cfar[0:w, h:h + 1] if allfar else zero1[0:w, 0:1]
                            S.op("act", lambda e, pm=pm, pt=pt, w=w, bias_ap=bias_ap: e.activation(
                                out=pt[0:w, 0:QB], in_=pm[0:w, 0:QB], func=AF.Exp, bias=bias_ap, scale=0.125),
                                reads=[pm, cfar, zero1], writes=[pt])

                            def f_pv(e, pt=pt, vc=vc, hl=hl, j=j, w=w, jt=jt, nkt=nkt):
                                ins = None
                                for ms in range(nsub):
                                    ins = e.matmul(pa[ms][0:npq, hl * 65:(hl + 1) * 65], lhsT=pt[0:w, ms * npq:(ms + 1) * npq],
                                                   rhs=vc[0:w, j, hl, :], start=(jt == 0 and hl == 0), stop=(jt == nkt - 1 and hl == 3))
                                return ins
                            pend.append((f_pv, [pt, vc]))
                            if len(pend) > 2:
                                fq, rq = pend.pop(0)
                                S.op("pe", fq, reads=rq, writes=pa[0:nsub])
                while pend:
                    fq, rq = pend.pop(0)
                    S.op("pe", fq, reads=rq, writes=pa[0:nsub])
                R = rs[bi % 2]
                for ms in range(nsub):
                    pv = pa[ms][0:npq, 0:260].rearrange("p (h d) -> p h d", d=65)
                    S.op("dve", lambda e, pv=pv, R=R, ms=ms, hg=hg: e.reciprocal(out=R[0:npq, ms, hg * 4:hg * 4 + 4], in_=pv[:, :, 64]),
                         reads=[pa[ms]], writes=[R])
                    S.op("dve", lambda e, pv=pv, R=R, ms=ms, hg=hg, A32=A32: e.tensor_tensor(
                        out=A32[0:npq, ms, hg * 256:(hg + 1) * 256].rearrange("p (h d) -> p h d", d=64), in0=pv[:, :, 0:64],
                        in1=R[0:npq, ms, hg * 4:hg * 4 + 4].unsqueeze(2).to_broadcast([npq, 4, 64]), op=ALU.mult),
                        reads=[pa[ms], R], writes=[A32])
                if hg == 0:
                    yield
            for ms in range(nsub):
                r0 = q0 + ms * npq
                S.dma("pool", X["a"], X["a"].t.ap()[r0:r0 + npq, :], A32, A32[0:npq, ms, :])

        prevB = None
        for bi in range(nblk):
            ga = genA(bi)
            for _st in range(nsub):
                next(ga)
                if prevB is not None:
                    next(prevB, None)
            for _ in ga:
                pass
            if prevB is not None:
                for _ in prevB:
                    pass
            prevB = genB(bi)
        for _ in prevB:
            pass

    group(False)
    group(True)


def phaseF1(g, ph):
    S, I, O, X = g.S, g.I, g.O, g.X
    T, PAST, NS, NT = g.T, g.PAST, g.NS, g.NT
    sb, ps = mk_alloc(g, ph)
    GC = 3072
    wg16 = sb([128, 8, GC], BF16, "wg16")
    wpa16 = sb([128, 4, D], BF16, "wpa16")
    wpr16 = sb([128, 8, D], BF16, "wpr16")
    wo16 = sb([128, 8, D], BF16, "wo16")
    for kc in range(8):
        S.dma("pool", wg16, wg16[:, kc, :], I["w_in"], I["w_in"].t.ap()[kc * 128:(kc + 1) * 128, C_GR:C_GR + GC])
        S.dma("pool", wpr16, wpr16[:, kc, :], I["w_pr"], I["w_pr"].t.ap()[kc * 128:(kc + 1) * 128, :])
        S.dma("pool", wo16, wo16[:, kc, :], I["w_o"], I["w_o"].t.ap()[kc * 128:(kc + 1) * 128, :])
    for kc in range(4):
        S.dma("pool", wpa16, wpa16[:, kc, :], I["w_pa"], I["w_pa"].t.ap()[kc * 128:(kc + 1) * 128, :])
    id16 = sb([128, 128], BF16, "id16")
    id32 = sb([128, 128], F32, "id32")
    S.dma("pool", id16, id16[:, :], I["c_ident"], I["c_ident"].t.ap()[:, :])
    S.dma("sp", id32, id32[:, :], I["c_ident"], I["c_ident"].t.ap()[:, :])
    DT = sb([128, 8, 128], F32, "DT")
    S.dma("sp", DT, DT[:, :, :], I["c_DT"], I["c_DT"].t.ap()[:, :, :])
    Gq = sb([128, 4, 128], F32, "Gq")
    S.dma("sp", Gq, Gq[:, :, :], I["c_Gq"], I["c_Gq"].t.ap()[:, :, :])
    Gk = sb([128, 2, 8], F32, "Gk")
    S.dma("sp", Gk, Gk[:, :, :], I["c_Gk"], I["c_Gk"].t.ap()[:, :, :])
    gC = sb([128, 2, 8, 64], F32, "gC")
    S.dma("sp", gC, gC[:, :, :, :], I["c_gC"], I["c_gC"].t.ap()[:, :, :, :])
    bc = sb([128, 3, D], F32, "bc")
    for i, nme in enumerate(["gn_g", "ln1_g", "ln1_b"]):
        S.dma("sp", bc, bc[:, i, :], I[nme], dap(I[nme].t, 0, [[0, 128], [1, D]]))
    eps = sb([128, 2], F32, "eps")
    S.op("dve", lambda e: e.memset(eps[:, 0:1], LN_EPS), writes=[eps])
    S.op("dve", lambda e: e.memset(eps[:, 1:2], GN_EPS), writes=[eps])
    ST = sb([128, 8, 64], F32, "ST")
    ST16 = sb([128, 8, 64], BF16, "ST16")
    S16 = sb([128, 4, 128], BF16, "S16")
    S32 = sb([128, 4, 128], F32, "S32")
    xT16_s = [sb([128, 8, 128], BF16, "xT16") for _ in range(2)]
    x32_s = [sb([128, D], F32, "x32") for _ in range(2)]
    a32_s = [sb([128, 512], F32, "a32") for _ in range(2)]
    a16_s = [sb([128, 512], BF16, "a16") for _ in range(2)]
    aT16_s = [sb([128, 4, 128], BF16, "aT16") for _ in range(2)]
    qrT_s = [sb([128, 4, 128], BF16, "qrT") for _ in range(2)]
    krT_s = [sb([128, 4, 128], BF16, "krT") for _ in range(2)]
    kr16_s = [sb([128, 512], BF16, "kr16") for _ in range(2)]
    vr16_s = [sb([128, 1024], BF16, "vr16") for _ in range(2)]
    qdec_s = [sb([128, 4, 128], BF16, "qdec") for _ in range(2)]
    kdec_s = [sb([128, 512], BF16, "kdec") for _ in range(2)]
    inD_s = [sb([128, 8, 128], BF16, "inD") for _ in range(2)]
    osq = sb([128, D], F32, "osq")
    on = sb([128, D], F32, "on")
    sl = sb([128, D], F32, "sl")
    st = sb([128, 8, 8], F32, "st")
    r16 = sb([128, D], BF16, "r16")
    rT16 = sb([128, 8, 128], BF16, "rT16")
    sgm = sb([128, D], F32, "sgm")
    mAm = sb([128, D], F32, "mAm")
    m16 = sb([128, D], BF16, "m16")
    mT16 = sb([128, 8, 128], BF16, "mT16")
    u = sb([128, D], F32, "u")
    u16 = sb([128, D], BF16, "u16")
    uT16 = sb([128, 8, 128], BF16, "uT16")
    st6 = sb([128, 2, 6], F32, "st6")
    mv = sb([128, 4], F32, "mv")
    pg = [ps([128, 512], F32, "pg") for _ in range(6)]
    pb = [ps([128, 1024], BF16, "pb") for _ in range(2)]
    pc = [0]

    def bank():
        pc[0] += 1
        return pg[pc[0] % 6]

    def transposes(src, ncols, dst, n, eng="dve"):
        nk = ncols // 128
        for g0 in range(0, nk, 8):
            pt = pb[(pc[0]) % 2]
            pc[0] += 1
            cnt = min(8, nk - g0)

            def f(e, pt=pt, g0=g0, cnt=cnt):
                ins = None
                for j in range(cnt):
                    ins = e.transpose(pt[:, j * 128:j * 128 + n], src[0:n, (g0 + j) * 128:(g0 + j + 1) * 128], id16[0:n, 0:n])
                return ins
            S.op("pe", f, reads=[src, id16], writes=[pt])
            pv = pt[:, 0:cnt * 128].rearrange("p (k t) -> p k t", t=128)[:, :, 0:n]
            if eng == "act":
                S.op("act", lambda e, pv=pv, g0=g0, cnt=cnt: e.activation(out=dst[:, g0:g0 + cnt, 0:n], in_=pv, func=AF.Copy),
                     reads=[pt], writes=[dst])
            else:
                S.op("dve", lambda e, pv=pv, g0=g0, cnt=cnt: e.tensor_copy(out=dst[:, g0:g0 + cnt, 0:n], in_=pv),
                     reads=[pt], writes=[dst])

    def ln_rows(src, n, gi, bi_, dst, epscol):
        for hf in range(2):
            S.op("dve", lambda e, hf=hf: e.bn_stats(out=st6[0:n, hf, :], in_=src[0:n, hf * 512:(hf + 1) * 512]),
                 reads=[src], writes=[st6])
        S.op("dve", lambda e: e.bn_aggr(out=mv[0:n, 0:2], in_=st6[0:n, :, :].rearrange("p a b -> p (a b)")), reads=[st6], writes=[mv])
        S.op("act", lambda e: e.activation(out=mv[0:n, 2:3], in_=mv[0:n, 1:2], func=AF.Sqrt, bias=eps[0:n, epscol:epscol + 1], scale=1.0),
             reads=[mv, eps], writes=[mv])
        S.op("dve", lambda e: e.reciprocal(out=mv[0:n, 3:4], in_=mv[0:n, 2:3]), reads=[mv], writes=[mv])
        S.op("dve", lambda e: e.tensor_scalar(out=dst[0:n, :], in0=src[0:n, :], scalar1=mv[0:n, 0:1], scalar2=mv[0:n, 3:4],
                                              op0=ALU.subtract, op1=ALU.mult), reads=[src, mv], writes=[dst])
        S.op("pool", lambda e: e.tensor_tensor(out=dst[0:n, :], in0=dst[0:n, :], in1=bc[0:n, gi, :], op=ALU.mult), reads=[dst, bc], writes=[dst])
        S.op("pool", lambda e: e.tensor_tensor(out=dst[0:n, :], in0=dst[0:n, :], in1=bc[0:n, bi_, :], op=ALU.add), reads=[dst, bc], writes=[dst])

    def write_state(dst):
        if 'ws' in SKIP:
            return
        for pr in range(4):
            pt = bank()
            S.op("pe", lambda e, pt=pt, pr=pr: e.transpose(pt[:, 0:128], ST[:, 2 * pr:2 * pr + 2, :].rearrange("p a b -> p (a b)"), id32[:, :]),
                 reads=[ST, id32], writes=[pt])
            S.op("dve", lambda e, pt=pt, pr=pr: e.tensor_copy(out=S32[:, pr, :], in_=pt[:, 0:128]), reads=[pt], writes=[S32])
        S.dma("sp", dst, dst.t.ap().rearrange("(pr p) e -> p pr e", p=128), S32, S32[:, :, :])

    S.op("dve", lambda e: e.memset(ST[:, :, :], 0.0), writes=[ST])
    S.op("dve", lambda e: e.memset(S16[:, :, :], 0.0), writes=[S16])
    pend_st = []

    def tile(tt):
        xT16 = xT16_s[tt % 2]
        x32 = x32_s[tt % 2]
        a32 = a32_s[tt % 2]
        a16 = a16_s[tt % 2]
        aT16 = aT16_s[tt % 2]
        qrT = qrT_s[tt % 2]
        krT = krT_s[tt % 2]
        kr16 = kr16_s[tt % 2]
        vr16 = vr16_s[tt % 2]
        qdec = qdec_s[tt % 2]
        kdec = kdec_s[tt % 2]
        inD = inD_s[tt % 2]
        smp = tt == NT
        n = NS if smp else 128
        tok0 = tt * 128
        ci = 1 if smp else 0
        if smp and 'ss' not in SKIP:
            write_state(O["stp"])
            S.dma("sp", S32, S32[:, :, :], I["sr"], I["sr"].t.ap().rearrange("(pr p) e -> p pr e", p=128))
            S.op("dve", lambda e: e.tensor_copy(out=S16[:, :, :], in_=S32[:, :, :]), reads=[S32], writes=[S16])
            for pr in range(4):
                pt = bank()
                S.op("pe", lambda e, pt=pt, pr=pr: e.transpose(pt[:, 0:128], S32[:, pr, :], id32[:, :]), reads=[S32, id32], writes=[pt])
                S.op("dve", lambda e, pt=pt, pr=pr: e.tensor_copy(out=ST[:, 2 * pr:2 * pr + 2, :].rearrange("p a b -> p (a b)"), in_=pt[:, 0:128]),
                     reads=[pt], writes=[ST])
        xsrc = I["xs"] if smp else I["xp"]
        xrow = 0 if smp else tok0
        S.dma("sp", x32, x32[0:n, :], xsrc, xsrc.t.ap()[xrow:xrow + n, :])
        S.dma("sp", xT16, xT16[:, :, 0:n], X["xT"], X["xT"].t.ap().rearrange("(k p) t -> p k t", p=128)[:, :, tok0:tok0 + n])
        S.dma("sp", a32, a32[0:n, :], X["a"], X["a"].t.ap()[tok0:tok0 + n, :])
        S.dma("sp", qrT, qrT[:, :, 0:n], X["qrT"], X["qrT"].t.ap()[:, :, tok0:tok0 + n])
        S.dma("sp", krT, krT[:, :, 0:n], X["krT"], X["krT"].t.ap()[:, :, tok0:tok0 + n])
        S.dma("sp", kr16, kr16[0:n, :], X["kr"], X["kr"].t.ap()[tok0:tok0 + n, :])
        S.dma("sp", vr16, vr16[0:n, :], X["vr"], X["vr"].t.ap()[tok0:tok0 + n, :])
        if CUT == 1:
            return
        S.op("pool", lambda e: e.tensor_tensor(out=qdec[:, :, 0:n], in0=qrT[:, :, 0:n], in1=Gq[:, :, 0:n], op=ALU.mult),
             reads=[qrT, Gq], writes=[qdec])
        while pend_st:
            oap_, iap_ = pend_st.pop(0)
            S.dma("pool", X["x1T"], oap_, uT16, iap_)
        if CUT == 11:
            return
        S.op("dve", lambda e, ci=ci: e.tensor_tensor(
            out=kdec[0:n, :].rearrange("p (h d) -> p h d", d=64), in0=kr16[0:n, :].rearrange("p (h d) -> p h d", d=64),
            in1=Gk[0:n, ci, :].unsqueeze(2).to_broadcast([n, 8, 64]), op=ALU.mult), reads=[kr16, Gk], writes=[kdec])
        if CUT == 12:
            return
        for hg in range(2):
            pis = [bank(), bank()]

            def f_in(e, pis=pis, hg=hg):
                ins = None
                for par in range(2):
                    for i2 in range(2):
                        h = hg * 4 + i2 * 2 + par
                        b0 = par * 64
                        ins = e.matmul(pis[par][0:n, i2 * 128:i2 * 128 + n], lhsT=krT[b0:b0 + 64, h // 2, 0:n],
                                       rhs=qrT[b0:b0 + 64, h // 2, 0:n], start=True, stop=True)
                return ins
            S.op("pe", f_in, reads=[krT, qrT], writes=pis)
            for par in range(2):
                h0 = hg * 4 + par
                S.op("dve", lambda e, pis=pis, par=par, h0=h0: e.tensor_tensor(
                    out=inD[0:n, h0:h0 + 3:2, 0:n], in0=pis[par][0:n, 0:256].rearrange("p (h t) -> p h t", t=128)[:, :, 0:n],
                    in1=DT[0:n, h0:h0 + 3:2, 0:n], op=ALU.mult), reads=[pis[par], DT], writes=[inD])
        if CUT in (13, 131):
            return
        po = [bank(), bank()]
        for hg in range(2):
            def f_o(e, hg=hg):
                ins = None
                for hl in range(4):
                    h = hg * 4 + hl
                    b0 = (h % 2) * 64
                    e.matmul(po[hg][0:n, hl * 128:(hl + 1) * 128], lhsT=inD[0:n, h, 0:n], rhs=vr16[0:n, h * 128:(h + 1) * 128],
                             start=True, stop=False)
                    ins = e.matmul(po[hg][0:n, hl * 128:(hl + 1) * 128], lhsT=qdec[b0:b0 + 64, h // 2, 0:n],
                                   rhs=S16[b0:b0 + 64, h // 2, :], start=False, stop=True)
                return ins
            S.op("pe", f_o, reads=[inD, vr16, qdec, S16], writes=[po[hg]])
        if CUT == 2:
            return
        pu = bank()

        def f_u(e):
            ins = None
            for h in range(8):
                ins = e.matmul(pu[:, h * 64:(h + 1) * 64], lhsT=vr16[0:n, h * 128:(h + 1) * 128], rhs=kdec[0:n, h * 64:(h + 1) * 64],
                               start=True, stop=True)
            return ins
        S.op("pe", f_u, reads=[vr16, kdec], writes=[pu])
        S.op("pool", lambda e, ci=ci: e.tensor_tensor(out=ST[:, :, :], in0=ST[:, :, :], in1=gC[:, ci, :, :], op=ALU.mult),
             reads=[ST, gC], writes=[ST])
        S.op("dve", lambda e: e.tensor_tensor(out=ST[:, :, :].rearrange("p a b -> p (a b)"), in0=ST[:, :, :].rearrange("p a b -> p (a b)"),
                                              in1=pu[:, :], op=ALU.add), reads=[ST, pu], writes=[ST])
        S.op("pool", lambda e: e.tensor_copy(out=ST16[:, :, :], in_=ST[:, :, :]), reads=[ST], writes=[ST16])
        if CUT == 3:
            return
        for hg in range(2):
            ov = po[hg][0:n, :].rearrange("p (h e) -> p h e", e=128)
            S.op("act", lambda e, hg=hg: e.activation(out=on[0:n, hg * 512:(hg + 1) * 512], in_=po[hg][0:n, :], func=AF.Copy),
                 reads=[po[hg]], writes=[on])
            S.op("dve", lambda e, hg=hg: e.tensor_reduce(out=st[0:n, 0, hg * 4:hg * 4 + 4],
                                                         in_=on[0:n, hg * 512:(hg + 1) * 512].rearrange("p (h e) -> p h e", e=128),
                                                         axis=AX.X, op=ALU.add), reads=[on], writes=[st])
            S.op("act", lambda e, hg=hg: e.activation(out=osq[0:n, hg * 512:(hg + 1) * 512], in_=po[hg][0:n, :], func=AF.Square),
                 reads=[po[hg]], writes=[osq])
            S.op("dve", lambda e, hg=hg: e.tensor_reduce(out=st[0:n, 1, hg * 4:hg * 4 + 4],
                                                         in_=osq[0:n, hg * 512:(hg + 1) * 512].rearrange("p (h e) -> p h e", e=128),
                                                         axis=AX.X, op=ALU.add), reads=[osq], writes=[st])
        S.op("dve", lambda e: e.tensor_scalar(out=st[0:n, 2, :], in0=st[0:n, 0, :], scalar1=1.0 / 128, scalar2=None, op0=ALU.mult),
             reads=[st], writes=[st])
        S.op("dve", lambda e: e.tensor_tensor(out=st[0:n, 3, :], in0=st[0:n, 2, :], in1=st[0:n, 2, :], op=ALU.mult), reads=[st], writes=[st])
        S.op("dve", lambda e: e.scalar_tensor_tensor(out=st[0:n, 4, :], in0=st[0:n, 1, :], scalar=1.0 / 128, in1=st[0:n, 3, :],
                                                     op0=ALU.mult, op1=ALU.subtract), reads=[st], writes=[st])
        S.op("act", lambda e: e.activation(out=st[0:n, 5, :], in_=st[0:n, 4, :], func=AF.Sqrt, bias=eps[0:n, 1:2], scale=1.0),
             reads=[st, eps], writes=[st])
        S.op("dve", lambda e: e.reciprocal(out=st[0:n, 6, :], in_=st[0:n, 5, :]), reads=[st], writes=[st])
        for hg in range(2):
            ov = po[hg][0:n, :].rearrange("p (h e) -> p h e", e=128)
            onv = on[0:n, hg * 512:(hg + 1) * 512].rearrange("p (h e) -> p h e", e=128)
            S.op("dve", lambda e, onv=onv, hg=hg: e.tensor_tensor(
                out=onv, in0=onv, in1=st[0:n, 2, hg * 4:hg * 4 + 4].unsqueeze(2).to_broadcast([n, 4, 128]), op=ALU.subtract),
                reads=[on, st], writes=[on])
            S.op("dve", lambda e, onv=onv, hg=hg: e.tensor_tensor(
                out=onv, in0=onv, in1=st[0:n, 6, hg * 4:hg * 4 + 4].unsqueeze(2).to_broadcast([n, 4, 128]), op=ALU.mult),
                reads=[on, st], writes=[on])
        S.op("pool", lambda e: e.tensor_tensor(out=on[0:n, :], in0=on[0:n, :], in1=bc[0:n, 0, :], op=ALU.mult), reads=[on, bc], writes=[on])
        if CUT == 4:
            return
        for hf in range(2):
            pq = bank()

            def f_gr(e, pq=pq, hf=hf):
                ins = None
                for kc in range(8):
                    ins = e.matmul(pq[0:n, :], lhsT=xT16[:, kc, 0:n], rhs=wg16[:, kc, hf * 512:(hf + 1) * 512], start=(kc == 0), stop=(kc == 7))
                return ins
            S.op("pe", f_gr, reads=[xT16, wg16], writes=[pq])
            S.op("act", lambda e, pq=pq, hf=hf: e.activation(out=sl[0:n, hf * 512:(hf + 1) * 512], in_=pq[0:n, :], func=AF.Silu),
                 reads=[pq], writes=[sl])
        S.op("dve", lambda e: e.tensor_tensor(out=r16[0:n, :], in0=on[0:n, :], in1=sl[0:n, :], op=ALU.mult), reads=[on, sl], writes=[r16])
        transposes(r16, 1024, rT16, n, "act")
        S.op("pool", lambda e: e.tensor_copy(out=a16[0:n, :], in_=a32[0:n, :]), reads=[a32], writes=[a16])
        transposes(a16, 512, aT16, n, "dve")
        if not smp:
            for hp in range(1):
                pt = pb[pc[0] % 2]
                pc[0] += 1

                def f_st(e, pt=pt):
                    ins = None
                    for pr in range(4):
                        ins = e.transpose(pt[:, pr * 128:(pr + 1) * 128], ST16[:, 2 * pr:2 * pr + 2, :].rearrange("p a b -> p (a b)"), id16[:, :])
                    return ins
                S.op("pe", f_st, reads=[ST16, id16], writes=[pt])
                S.op("act", lambda e, pt=pt: e.activation(out=S16[:, :, :], in_=pt[:, 0:512].rearrange("p (k t) -> p k t", t=128), func=AF.Copy),
                     reads=[pt], writes=[S16])
        if CUT == 5:
            return
        for bi_, (gcol, wmat, nkc, actT) in enumerate([(1024, wpa16, 4, aT16), (2048, wpr16, 8, rT16)]):
            for hf in range(2):
                pgt = bank()
                pp = bank()

                def f_g(e, pgt=pgt, hf=hf, gcol=gcol):
                    ins = None
                    c0 = gcol + hf * 512
                    for kc in range(8):
                        ins = e.matmul(pgt[0:n, :], lhsT=xT16[:, kc, 0:n], rhs=wg16[:, kc, c0:c0 + 512], start=(kc == 0), stop=(kc == 7))
                    return ins
                S.op("pe", f_g, reads=[wg16, xT16], writes=[pgt])

                def f_p(e, pp=pp, hf=hf, wmat=wmat, nkc=nkc, actT=actT):
                    ins = None
                    for kc in range(nkc):
                        ins = e.matmul(pp[0:n, :], lhsT=actT[:, kc, 0:n], rhs=wmat[:, kc, hf * 512:(hf + 1) * 512],
                                       start=(kc == 0), stop=(kc == nkc - 1))
                    return ins
                S.op("pe", f_p, reads=[wmat, actT], writes=[pp])
                cs_ = slice(hf * 512, (hf + 1) * 512)
                S.op("act", lambda e, pgt=pgt, cs_=cs_: e.activation(out=sgm[0:n, cs_], in_=pgt[0:n, :], func=AF.Sigmoid),
                     reads=[pgt], writes=[sgm])
                if bi_ == 0:
                    S.op("dve", lambda e, pp=pp, cs_=cs_: e.tensor_tensor(out=mAm[0:n, cs_], in0=sgm[0:n, cs_], in1=pp[0:n, :], op=ALU.mult),
                         reads=[sgm, pp], writes=[mAm])
                else:
                    S.op("dve", lambda e, pp=pp, cs_=cs_: e.tensor_tensor(out=sgm[0:n, cs_], in0=sgm[0:n, cs_], in1=pp[0:n, :], op=ALU.mult),
                         reads=[sgm, pp], writes=[sgm])
                    S.op("pool", lambda e, cs_=cs_: e.tensor_tensor(out=m16[0:n, cs_], in0=sgm[0:n, cs_], in1=mAm[0:n, cs_], op=ALU.add),
                         reads=[sgm, mAm], writes=[m16])
        transposes(m16, 1024, mT16, n, "act")
        for hf in range(2):
            pw = bank()

            def f_w(e, pw=pw, hf=hf):
                ins = None
                for kc in range(8):
                    ins = e.matmul(pw[0:n, :], lhsT=mT16[:, kc, 0:n], rhs=wo16[:, kc, hf * 512:(hf + 1) * 512], start=(kc == 0), stop=(kc == 7))
                return ins
            S.op("pe", f_w, reads=[mT16, wo16], writes=[pw])
            S.op("dve", lambda e, pw=pw, hf=hf: e.scalar_tensor_tensor(
                out=u[0:n, hf * 512:(hf + 1) * 512], in0=x32[0:n, hf * 512:(hf + 1) * 512], scalar=ALPHA, in1=pw[0:n, :],
                op0=ALU.mult, op1=ALU.add), reads=[x32, pw], writes=[u])
        ln_rows(u, n, 1, 2, u, 0)
        S.dma("pool", X["x1"], X["x1"].t.ap()[tok0:tok0 + n, :], u, u[0:n, :])
        S.op("act", lambda e: e.activation(out=u16[0:n, :], in_=u[0:n, :], func=AF.Copy), reads=[u], writes=[u16])
        transposes(u16, 1024, uT16, n, "dve")
        pend_st.append((X["x1T"].t.ap().rearrange("(k p) t -> p k t", p=128)[:, :, tok0:tok0 + n], uT16[:, :, 0:n]))
    for tt in range(NT + 1):
        tile(tt)
    while pend_st:
        oap_, iap_ = pend_st.pop(0)
        S.dma("pool", X["x1T"], oap_, uT16, iap_)
    write_state(O["sts"])


def phaseF2(g, ph):
    S, I, O, X = g.S, g.I, g.O, g.X
    T, PAST, NS, NT = g.T, g.PAST, g.NS, g.NT
    sb, ps = mk_alloc(g, ph)
    NFC = DFF // 128
    wg = sb([128, 8, DFF], BF16, "wg")
    wu = sb([128, 8, DFF], BF16, "wu")
    wd = sb([128, NFC, D], BF16, "wd")
    for kc in range(8):
        S.dma("pool", wg, wg[:, kc, :], I["w_gate"], I["w_gate"].t.ap()[kc * 128:(kc + 1) * 128, :])
        S.dma("pool", wu, wu[:, kc, :], I["w_up"], I["w_up"].t.ap()[kc * 128:(kc + 1) * 128, :])
    for fc in range(NFC):
        S.dma("pool", wd, wd[:, fc, :], I["w_down"], I["w_down"].t.ap()[fc * 128:(fc + 1) * 128, :])
    bc = sb([128, 2, D], F32, "bc2")
    for i, nme in enumerate(["ln2_g", "ln2_b"]):
        S.dma("sp", bc, bc[:, i, :], I[nme], dap(I[nme].t, 0, [[0, 128], [1, D]]))
    eps = sb([128, 1], F32, "eps2")
    S.op("dve", lambda e: e.memset(eps[:, :], LN_EPS), writes=[eps])
    BLK = 512
    x1T = [sb([128, 8, BLK], BF16, "x1T") for _ in range(2)]
    hT = [sb([128, NFC, BLK], BF16, "hT") for _ in range(1)]
    sgt = [sb([128, BLK], F32, "sgt") for _ in range(2)]
    x1 = [sb([128, D], F32, "x1") for _ in range(2)]
    st6 = [sb([128, 2, 6], F32, "st6") for _ in range(2)]
    mv = [sb([128, 4], F32, "mv") for _ in range(2)]
    pgu = [ps([128, 512], F32, "pgu") for _ in range(4)]
    py = [ps([128, 512], F32, "py") for _ in range(4)]
    cn = [0, 0, 0]
    blocks = [(i * BLK, BLK, False) for i in range(T // BLK)] + [(T, NS, True)]
    for bi, (t0, nb, smp) in enumerate(blocks):
        xb = x1T[bi % 2]
        S.dma("sp", xb, xb[:, :, 0:nb], X["x1T"], X["x1T"].t.ap().rearrange("(k p) t -> p k t", p=128)[:, :, t0:t0 + nb])
        H = hT[0]
        for fc in range(NFC):
            pg_ = pgu[cn[0] % 4]
            pu_ = pgu[(cn[0] + 1) % 4]
            cn[0] += 2

            def f_gu(e, pg_=pg_, pu_=pu_, fc=fc, xb=xb, nb=nb):
                ins = None
                for kc in range(8):
                    e.matmul(pg_[:, 0:nb], lhsT=wg[:, kc, fc * 128:(fc + 1) * 128], rhs=xb[:, kc, 0:nb], start=(kc == 0), stop=(kc == 7))
                for kc in range(8):
                    ins = e.matmul(pu_[:, 0:nb], lhsT=wu[:, kc, fc * 128:(fc + 1) * 128], rhs=xb[:, kc, 0:nb], start=(kc == 0), stop=(kc == 7))
                return ins
            S.op("pe", f_gu, reads=[wg, wu, xb], writes=[pg_, pu_])
            sg_ = sgt[fc % 2]
            S.op("act", lambda e, pg_=pg_, sg_=sg_, nb=nb: e.activation(out=sg_[:, 0:nb], in_=pg_[:, 0:nb], func=AF.Silu),
                 reads=[pg_], writes=[sg_])
            S.op("dve", lambda e, pu_=pu_, sg_=sg_, fc=fc, nb=nb, H=H: e.tensor_tensor(out=H[:, fc, 0:nb], in0=sg_[:, 0:nb], in1=pu_[:, 0:nb],
                                                                                    op=ALU.mult), reads=[sg_, pu_], writes=[H])
        for ms in range((nb + 127) // 128):
            n = min(128, nb - ms * 128)
            r0 = t0 + ms * 128
            xr = x1[cn[1] % 2]
            s6 = st6[cn[1] % 2]
            m_ = mv[cn[1] % 2]
            cn[1] += 1
            S.dma("sp", xr, xr[0:n, :], X["x1"], X["x1"].t.ap()[r0:r0 + n, :])
            for hf in range(2):
                pyy = py[cn[2] % 4]
                cn[2] += 1

                def f_d(e, pyy=pyy, hf=hf, ms=ms, n=n, H=H):
                    ins = None
                    for fc in range(NFC):
                        ins = e.matmul(pyy[0:n, :], lhsT=H[:, fc, ms * 128:ms * 128 + n], rhs=wd[:, fc, hf * 512:(hf + 1) * 512],
                                       start=(fc == 0), stop=(fc == NFC - 1))
                    return ins
                S.op("pe", f_d, reads=[H, wd], writes=[pyy])
                S.op("dve", lambda e, pyy=pyy, hf=hf, n=n, xr=xr: e.scalar_tensor_tensor(
                    out=xr[0:n, hf * 512:(hf + 1) * 512], in0=xr[0:n, hf * 512:(hf + 1) * 512], scalar=ALPHA, in1=pyy[0:n, :],
                    op0=ALU.mult, op1=ALU.add), reads=[xr, pyy], writes=[xr])
                S.op("dve", lambda e, hf=hf, n=n, xr=xr, s6=s6: e.bn_stats(out=s6[0:n, hf, :], in_=xr[0:n, hf * 512:(hf + 1) * 512]),
                     reads=[xr], writes=[s6])
            S.op("dve", lambda e, n=n, s6=s6, m_=m_: e.bn_aggr(out=m_[0:n, 0:2], in_=s6[0:n, :, :].rearrange("p a b -> p (a b)")),
                 reads=[s6], writes=[m_])
            S.op("act", lambda e, n=n, m_=m_: e.activation(out=m_[0:n, 2:3], in_=m_[0:n, 1:2], func=AF.Sqrt, bias=eps[0:n, 0:1], scale=1.0),
                 reads=[m_, eps], writes=[m_])
            S.op("dve", lambda e, n=n, m_=m_: e.reciprocal(out=m_[0:n, 3:4], in_=m_[0:n, 2:3]), reads=[m_], writes=[m_])
            S.op("dve", lambda e, n=n, m_=m_, xr=xr: e.tensor_scalar(out=xr[0:n, :], in0=xr[0:n, :], scalar1=m_[0:n, 0:1], scalar2=m_[0:n, 3:4],
                                                                     op0=ALU.subtract, op1=ALU.mult), reads=[xr, m_], writes=[xr])
            S.op("pool", lambda e, n=n, xr=xr: e.tensor_tensor(out=xr[0:n, :], in0=xr[0:n, :], in1=bc[0:n, 0, :], op=ALU.mult),
                 reads=[xr, bc], writes=[xr])
            S.op("pool", lambda e, n=n, xr=xr: e.tensor_tensor(out=xr[0:n, :], in0=xr[0:n, :], in1=bc[0:n, 1, :], op=ALU.add),
                 reads=[xr, bc], writes=[xr])
            yo = O["ys"] if smp else O["yp"]
            yr = 0 if smp else r0
            S.dma("pool", yo, yo.t.ap()[yr:yr + n, :], xr, xr[0:n, :])


def host_consts(T, PAST, NS):
    pos = np.concatenate([np.arange(T), PAST + np.arange(NS)]).astype(np.float64)
    inv = 10000.0 ** (-np.arange(32, dtype=np.float64) / 32)
    ang = pos[:, None] * inv[None, :]
    w = np.arange(896)
    bk = t5_bucket_np(127 - w)
    oh = (bk[None, :] == np.arange(32)[:, None]).astype(np.float32)
    gam = 1.0 - 2.0 ** (-5.0 - np.arange(8, dtype=np.float64))
    m = np.arange(128)
    diff = m[None, :] - m[:, None]
    DT = np.where(diff[:, None, :] >= 0, gam[None, :, None] ** np.maximum(diff[:, None, :], 0), 0.0) / 8.0
    hh = (np.arange(4)[None, :] * 2 + (np.arange(128)[:, None] // 64))
    Gq = gam[hh][:, :, None] ** (m[None, None, :] + 1.0)
    Gk = np.zeros((128, 2, 8))
    Gk[:, 0, :] = gam[None, :] ** (127.0 - m[:, None]) / 8.0
    Gk[:NS, 1, :] = gam[None, :] ** (NS - 1.0 - m[:NS, None]) / 8.0
    gC = np.zeros((128, 2, 8, 64))
    gC[:, 0] = (gam ** 128.0)[None, :, None]
    gC[:, 1] = (gam ** float(NS))[None, :, None]
    f32 = lambda a: np.ascontiguousarray(a, dtype=np.float32)
    return {"c_DT": f32(DT), "c_Gq": f32(Gq), "c_Gk": f32(Gk), "c_gC": f32(gC), "c_ident": np.eye(128, dtype=np.float32), "c_anti": np.ascontiguousarray(np.eye(128, dtype=np.float32)[::-1]), "c_oh": oh, "c_cos": np.cos(ang).astype(np.float32),
            "c_sin": np.sin(ang).astype(np.float32)}


def make_in_maps(inp, T, PAST, NS, ncores):
    cst = host_consts(T, PAST, NS)
    f = lambda a: np.ascontiguousarray(a, dtype=np.float32)
    maps = []
    for b in range(ncores):
        m = {"xp": f(inp["x_prompt"][b]), "xs": f(inp["x_sample"][b]),
             "ck": f(inp["cache_k"][0, b].reshape(PAST, 512)), "cv": f(inp["cache_v"][0, b].reshape(PAST, 512)),
             "cik": f(inp["cache_idx_k"][0, b]), "sr": f(inp["state_ret"][0, b].reshape(512, 128)),
             "w_in": f(inp["w_in"][0]), "idx_g": f(inp["idx_k_norm_g"]), "idx_b": f(inp["idx_k_norm_b"]),
             "t5": f(inp["t5_bias"]), "gn_g": f(inp["ret_gn_g"]), "w_pa": f(inp["w_pa"][0]), "w_pr": f(inp["w_pr"][0]),
             "w_o": f(inp["w_o"][0]), "ln1_g": f(inp["ln1_g"]), "ln1_b": f(inp["ln1_b"]), "w_gate": f(inp["w_gate"][0]),
             "w_up": f(inp["w_up"][0]), "w_down": f(inp["w_down"][0]), "ln2_g": f(inp["ln2_g"]), "ln2_b": f(inp["ln2_b"])}
        m.update(cst)
        maps.append(m)
    return maps


def assemble(results, T, NS):
    B = len(results)
    st = lambda k: np.stack([np.asarray(r[k], dtype=np.float32) for r in results])
    yp = st("yp")
    ys = st("ys")
    kp = st("kp").reshape(1, B, T, 8, 64)
    vp = st("vp").reshape(1, B, T, 8, 64)
    ikp = st("ikp").reshape(1, B, T, 64)
    stp = st("stp").reshape(1, B, 8, 64, 128)
    ks = st("ks").reshape(1, B, NS, 8, 64)
    vs = st("vs").reshape(1, B, NS, 8, 64)
    iks = st("iks").reshape(1, B, NS, 64)
    sts = st("sts").reshape(1, B, 8, 64, 128)
    return (yp, ys, kp, vp, ikp, stp, ks, vs, iks, sts)


def kernel(**inputs):
    T, PAST, NS = 8192, 4096, 32
    nc = build_program(T, PAST, NS)
    maps = make_in_maps(inputs, T, PAST, NS, 8)
    res = run_bass_kernel_spmd(nc, maps, core_ids=list(range(8)))
    return assemble(res.results, T, NS)
```
